# Optimizing a Trainium2 kernel written in Bass

```python
import jax
import jax.numpy as jnp
from jax import lax
import numpy as np


D_MODEL = 1024
BATCH = 4
SEQ = 4096
DEPTH = 2

HEAD_DIM = 64
A_WIDTH = D_MODEL // 2
A_HEADS = A_WIDTH // HEAD_DIM
CHUNK = 128
B_WIDTH = D_MODEL // 2
CONV_WIDTH = 31
IN_WIDTH = 2 * A_WIDTH + 2 * B_WIDTH
ATTN_HEADS = D_MODEL // HEAD_DIM
DILATED_PATTERNS = ((128, 1), (512, 4), (2048, 16))
BLOCK = 128
ROPE_THETA = 10000.0
D_FF = 2816
FFN_CONV_WIDTH = 3
EPS = 1e-6
NEG = -1e30

kernel_name = 'hybrid_gmlp_conformer_dilated_attn_block'


def rms_norm(x, g):
    xf = x.astype(jnp.float32)
    y = xf * lax.rsqrt(jnp.mean(xf * xf, axis=-1, keepdims=True) + EPS)
    return (y * g.astype(jnp.float32)).astype(x.dtype)


def layer_norm(x, g, b):
    xf = x.astype(jnp.float32)
    mu = jnp.mean(xf, axis=-1, keepdims=True)
    var = jnp.mean(jnp.square(xf - mu), axis=-1, keepdims=True)
    y = (xf - mu) * lax.rsqrt(var + EPS)
    return (y * g.astype(jnp.float32) + b.astype(jnp.float32)).astype(x.dtype)


def causal_dwconv(x, w, b):
    k = w.shape[0]
    y = lax.conv_general_dilated(
        x, w[:, None, :].astype(x.dtype), window_strides=(1,),
        padding=[(k - 1, 0)], dimension_numbers=('NWC', 'WIO', 'NWC'),
        feature_group_count=x.shape[-1])
    return y + b.astype(x.dtype)


def rotary(x, pos):
    half = x.shape[-1] // 2
    inv = ROPE_THETA ** (-jnp.arange(half, dtype=jnp.float32) / half)
    ang = pos.astype(jnp.float32)[:, None] * inv[None, :]
    cos = jnp.cos(ang)[None, :, None, :]
    sin = jnp.sin(ang)[None, :, None, :]
    xf = x.astype(jnp.float32)
    x1, x2 = xf[..., :half], xf[..., half:]
    out = jnp.concatenate([x1 * cos - x2 * sin, x2 * cos + x1 * sin], axis=-1)
    return out.astype(x.dtype)


def chunked_spatial_gating(z, ln_g, ln_b, w_s, b_s):
    u, v = jnp.split(z, 2, axis=-1)
    v = layer_norm(v, ln_g, ln_b)
    bn, s, _ = v.shape
    v = v.reshape(bn, s // CHUNK, CHUNK, A_HEADS, HEAD_DIM)
    causal = jnp.tril(jnp.ones((CHUNK, CHUNK), dtype=bool))
    w = jnp.where(causal, w_s, jnp.zeros_like(w_s))
    mixed = jnp.einsum('hts,bcshd->bcthd', w, v) + b_s.T[None, None, :, :, None]
    return u * mixed.reshape(bn, s, A_WIDTH)


def conformer_conv(z, conv_w, conv_b, ln_g, ln_b):
    a, g = jnp.split(z, 2, axis=-1)
    h = a * jax.nn.sigmoid(g)
    h = causal_dwconv(h, conv_w, conv_b)
    h = layer_norm(h, ln_g, ln_b)
    return jax.nn.silu(h)


def dilated_branch(q, k, v, window, dilation):
    bn, s, h, dh = q.shape
    span = dilation * BLOCK
    s_pad = -(-s // span) * span
    seq_len = s_pad // dilation
    nb = seq_len // BLOCK
    reach = window // dilation

    def strided(t):
        t = jnp.pad(t, [(0, 0), (0, s_pad - s), (0, 0), (0, 0)])
        t = t.reshape(bn, seq_len, dilation, h, dh).transpose(0, 2, 1, 3, 4)
        return t.reshape(bn, dilation, nb, BLOCK, h, dh)

    def with_prev(t):
        prev = jnp.pad(t, [(0, 0), (0, 0), (1, 0), (0, 0), (0, 0), (0, 0)])[:, :, :-1]
        return jnp.concatenate([prev, t], axis=3)

    qs = strided(q)
    kw = with_prev(strided(k))
    vw = with_prev(strided(v))
    qi = jnp.arange(BLOCK)[:, None]
    kj = jnp.arange(2 * BLOCK)[None, :]
    dist = BLOCK + qi - kj
    band = (dist >= 0) & (dist <= reach)
    blk = jnp.arange(nb)[:, None, None]
    valid = band[None] & ((blk > 0) | (kj[None] >= BLOCK))

    scores = jnp.einsum('brnqhd,brnkhd->brnhqk', qs, kw).astype(jnp.float32)
    scores = scores * (HEAD_DIM ** -0.5)
    scores = jnp.where(valid[None, None, :, None], scores, NEG)
    m = jnp.max(scores, axis=-1, keepdims=True)
    p = jnp.exp(scores - m)
    l = jnp.sum(p, axis=-1, keepdims=True)
    o = jnp.einsum('brnhqk,brnkhd->brnqhd', p / l, vw.astype(jnp.float32))
    lse = (m + jnp.log(l))[..., 0]

    o = o.reshape(bn, dilation, seq_len, h, dh).transpose(0, 2, 1, 3, 4)
    o = o.reshape(bn, s_pad, h, dh)[:, :s]
    lse = lse.transpose(0, 1, 2, 4, 3).reshape(bn, dilation, seq_len, h)
    lse = lse.transpose(0, 2, 1, 3).reshape(bn, s_pad, h)[:, :s]
    return o, lse


def dilated_attention(q, k, v):
    outs, lses = [], []
    for window, dilation in DILATED_PATTERNS:
        o, lse = dilated_branch(q, k, v, window, dilation)
        outs.append(o)
        lses.append(lse)
    wts = jax.nn.softmax(jnp.stack(lses, axis=0), axis=0)
    return jnp.einsum('pbsh,pbshd->bshd', wts, jnp.stack(outs, axis=0))


def conv_ffn(x, norm_g, w_up, conv_w, conv_b, w_down):
    h = rms_norm(x, norm_g)
    u = causal_dwconv(h @ w_up, conv_w, conv_b)
    gate, val = jnp.split(u, 2, axis=-1)
    return x + (jax.nn.silu(gate) * val) @ w_down


def setup_inputs(seed: int = 0) -> dict:
    key = jax.random.key(seed)
    keys = iter(jax.random.split(key, 32))
    n_even = (DEPTH + 1) // 2
    n_odd = DEPTH // 2
    f32 = jnp.float32

    def nrm(shape, scale):
        return jax.random.normal(next(keys), shape, f32) * scale

    def gain(shape):
        return 1.0 + nrm(shape, 0.02)

    return {
        'x': nrm((BATCH, SEQ, D_MODEL), 1.0),
        'even_norm_g': gain((n_even, D_MODEL)),
        'even_w_in': nrm((n_even, D_MODEL, IN_WIDTH), D_MODEL ** -0.5),
        'even_b_in': nrm((n_even, IN_WIDTH), 0.02),
        'even_v_ln_g': gain((n_even, A_WIDTH)),
        'even_v_ln_b': nrm((n_even, A_WIDTH), 0.02),
        'even_w_s': nrm((n_even, A_HEADS, CHUNK, CHUNK), CHUNK ** -0.5),
        'even_b_s': gain((n_even, A_HEADS, CHUNK)),
        'even_conv_w': nrm((n_even, CONV_WIDTH, B_WIDTH), CONV_WIDTH ** -0.5),
        'even_conv_b': nrm((n_even, B_WIDTH), 0.02),
        'even_conv_ln_g': gain((n_even, B_WIDTH)),
        'even_conv_ln_b': nrm((n_even, B_WIDTH), 0.02),
        'even_w_out': nrm((n_even, A_WIDTH + B_WIDTH, D_MODEL), (A_WIDTH + B_WIDTH) ** -0.5),
        'odd_norm_g': gain((n_odd, D_MODEL)),
        'odd_w_qkv': nrm((n_odd, D_MODEL, 3 * D_MODEL), D_MODEL ** -0.5),
        'odd_w_o': nrm((n_odd, D_MODEL, D_MODEL), D_MODEL ** -0.5),
        'ffn_norm_g': gain((DEPTH, D_MODEL)),
        'ffn_w_up': nrm((DEPTH, D_MODEL, 2 * D_FF), D_MODEL ** -0.5),
        'ffn_conv_w': nrm((DEPTH, FFN_CONV_WIDTH, 2 * D_FF), FFN_CONV_WIDTH ** -0.5),
        'ffn_conv_b': nrm((DEPTH, 2 * D_FF), 0.02),
        'ffn_w_down': nrm((DEPTH, D_FF, D_MODEL), D_FF ** -0.5),
        'final_norm_g': gain((D_MODEL,)),
    }


def reference(x, even_norm_g, even_w_in, even_b_in, even_v_ln_g, even_v_ln_b,
              even_w_s, even_b_s, even_conv_w, even_conv_b, even_conv_ln_g,
              even_conv_ln_b, even_w_out, odd_norm_g, odd_w_qkv, odd_w_o,
              ffn_norm_g, ffn_w_up, ffn_conv_w, ffn_conv_b, ffn_w_down,
              final_norm_g):
    bn, s, _ = x.shape
    pos = jnp.arange(s)
    for i in range(DEPTH):
        j = i // 2
        if i % 2 == 0:
            h = rms_norm(x, even_norm_g[j])
            z = h @ even_w_in[j] + even_b_in[j]
            za = jax.nn.gelu(z[..., :2 * A_WIDTH])
            zb = z[..., 2 * A_WIDTH:]
            ya = chunked_spatial_gating(za, even_v_ln_g[j], even_v_ln_b[j],
                                        even_w_s[j], even_b_s[j])
            yb = conformer_conv(zb, even_conv_w[j], even_conv_b[j],
                                even_conv_ln_g[j], even_conv_ln_b[j])
            x = x + jnp.concatenate([ya, yb], axis=-1) @ even_w_out[j]
        else:
            h = rms_norm(x, odd_norm_g[j])
            qkv = (h @ odd_w_qkv[j]).reshape(bn, s, 3, ATTN_HEADS, HEAD_DIM)
            q = rotary(qkv[:, :, 0], pos)
            k = rotary(qkv[:, :, 1], pos)
            v = qkv[:, :, 2]
            o = dilated_attention(q, k, v).astype(x.dtype).reshape(bn, s, D_MODEL)
            x = x + o @ odd_w_o[j]
        x = conv_ffn(x, ffn_norm_g[i], ffn_w_up[i], ffn_conv_w[i], ffn_conv_b[i],
                     ffn_w_down[i])
    return rms_norm(x, final_norm_g)
```

```python
from contextlib import ExitStack
import numpy as np
import ml_dtypes
import concourse.bass as bass
import concourse.mybir as mybir
from concourse.bass_utils import run_bass_kernel_spmd

F32 = mybir.dt.float32
BF16 = mybir.dt.bfloat16
AF = mybir.ActivationFunctionType
ALU = mybir.AluOpType

NCORES = 8
_NC_CACHE = {}
D = 1024
DC = 8
T = 2048
TT = 512
NT = T // TT
DFF = 2816
FC = DFF // 128
EPS = 1e-6


class Buf:
    __slots__ = ("name", "w", "r")

    def __init__(self, name):
        self.name = name
        self.w = None
        self.r = {}


class Emit:
    def __init__(self, nc, stack):
        self.nc = nc
        self.stack = stack
        self.eng = {"pe": nc.tensor, "act": nc.scalar, "dve": nc.vector,
                    "pool": nc.gpsimd, "sp": nc.sync}
        self.sem = {}
        self.cnt = {}
        self.seen = {k: {} for k in self.eng}
        for k in ("pe", "act", "dve", "pool"):
            self._mksem(k)
        self.nbuf = 0
        self.defer = None

    def _mksem(self, key):
        self.sem[key] = self.stack.enter_context(self.nc.semaphore("s_" + key))
        self.cnt[key] = 0

    def buf(self, name=None):
        self.nbuf += 1
        return Buf(name or "b%d" % self.nbuf)

    def _wait(self, e, need, skipkey=None):
        eng = self.eng[e]
        seen = self.seen[e]
        for k, c in need.items():
            if (e == "pe" and k == "pe") or k == skipkey:
                continue
            if seen.get(k, 0) < c:
                eng.wait_ge(self.sem[k], c)
                seen[k] = c

    def op(self, e, fn, reads=(), writes=(), dmasem=None, group=False, inc=16):
        if self.defer is not None:
            self.defer.append((e, fn, list(reads), list(writes), dmasem, group, inc))
            return None
        need = {}

        def add(ev):
            if ev is None:
                return
            k, c = ev
            if need.get(k, 0) < c:
                need[k] = c
        for b in reads:
            add(b.w)
        for b in writes:
            add(b.w)
            for k, c in b.r.items():
                add((k, c))
        self._wait(e, need, skipkey=(dmasem if group else None))
        ins = fn(self.eng[e])
        if dmasem is not None:
            if dmasem not in self.sem:
                self._mksem(dmasem)
            key, inc = dmasem, inc
        else:
            key, inc = e, 1
        ins.then_inc(self.sem[key], inc)
        self.cnt[key] += inc
        ev = (key, self.cnt[key])
        for b in reads:
            if b.r.get(key, 0) < ev[1]:
                b.r[key] = ev[1]
        for b in writes:
            b.w = ev
            b.r = {}
        return ins

    def replay(self, ops, n):
        while n > 0 and ops:
            a = ops.pop(0)
            self.op(a[0], a[1], reads=a[2], writes=a[3], dmasem=a[4], group=a[5], inc=a[6])
            n -= 1

    def finish(self, e, bufs):
        need = {}
        for b in bufs:
            if b.w is not None:
                k, c = b.w
                need[k] = max(need.get(k, 0), c)
            for k, c in b.r.items():
                need[k] = max(need.get(k, 0), c)
        self._wait(e, need)


class Ring:
    uid = 0

    def __init__(self, E, nc, st, name, shape, dtype, n, psum=False):
        self.t = []
        self.b = []
        Ring.uid += 1
        name = "%s_r%d_" % (name, Ring.uid)
        for i in range(n):
            if psum:
                t = st.enter_context(nc.psum_tensor("%s%d" % (name, i), shape, dtype))
            else:
                t = st.enter_context(nc.sbuf_tensor("%s%d" % (name, i), shape, dtype))
            self.t.append(t)
            self.b.append(E.buf("%s%d" % (name, i)))
        self.i = 0
        self.n = n

    def next(self):
        i = self.i
        self.i = (i + 1) % self.n
        return self.t[i], self.b[i], i


class Ctx:
    pass


def make_ctx(nc, st):
    C = Ctx()
    C.nc = nc
    C.gst = st
    C.st = st
    C.E = Emit(nc, st)
    C.pf = "g_"
    C.sb = lambda n, s, d: C.st.enter_context(nc.sbuf_tensor(C.pf + n, s, d))
    C.gsb = lambda n, s, d: C.gst.enter_context(nc.sbuf_tensor(n, s, d))
    C.nphase = 0
    return C


def phase_begin(C, psum=True):
    E, nc = C.E, C.nc
    C.nphase += 1
    C.st = ExitStack()
    C.st.__enter__()
    st = C.st
    pf = "p%d_" % C.nphase
    if psum == "ffn":
        C.pW = Ring(E, nc, st, pf + "pW", [128, 2 * TT], F32, 3, psum=True)
        C.pS = Ring(E, nc, st, pf + "pS", [128, 512], F32, 2, psum=True)
    elif psum:
        C.pA = Ring(E, nc, st, pf + "pA", [128, 512], F32, 3, psum=True)
        C.pB = Ring(E, nc, st, pf + "pB", [128, 512], F32, 3, psum=True)
        C.pS = Ring(E, nc, st, pf + "pS", [128, 512], F32, 2, psum=True)
    C.pf = pf
    C.ones = C.sb("ones", [128, 128], BF16)
    C.bones = E.buf("ones")
    E.op("pool", lambda e: e.memset(C.ones[:], 1.0), writes=[C.bones])
    C.epsb = C.sb("epsb", [128, 1], F32)
    C.bepsb = E.buf()
    E.op("pool", lambda e: e.memset(C.epsb[:], EPS), writes=[C.bepsb])
    C.sq = Ring(E, nc, st, pf + "sq", [128, TT], BF16, 3)
    C.pf = pf


def phase_end(C):
    barrier(C)
    C.st.__exit__(None, None, None)
    C.st = C.gst


def barrier(C):
    E = C.E
    need = {k: c for k, c in E.cnt.items() if c > 0}
    for e in E.eng:
        E._wait(e, dict(need))


def rms_stats(C, xsrc, xb, ncols, rstd, brstd):
    E = C.E
    ps, bps, _ = C.pS.next()
    for c in range(DC):
        sq, bsq, _ = C.sq.next()
        E.op("act", lambda e, c=c, sq=sq: e.activation(sq[:, :ncols], xsrc(c), AF.Square),
             reads=xb, writes=[bsq])
        E.op("pe", lambda e, c=c, sq=sq: e.matmul(ps[:, :ncols], lhsT=C.ones[:], rhs=sq[:, :ncols],
                                                   start=(c == 0), stop=(c == DC - 1)),
             reads=[bsq, C.bones], writes=[bps])
    E.op("act", lambda e: e.activation(rstd[:, :ncols], ps[:, :ncols], AF.Sqrt,
                                        bias=C.epsb[:, 0:1], scale=1.0 / D),
         reads=[bps, C.bepsb], writes=[brstd])
    E.op("dve", lambda e: e.reciprocal(rstd[:, :ncols], rstd[:, :ncols]), reads=[brstd], writes=[brstd])


def ffn_phase(C, xres, XH, bx, bxh, W, final_g=None, outT=None):
    nc, st, E, sb = C.nc, C.st, C.E, C.sb
    HC = 2
    ST = 2 * TT
    GW = 256
    g = sb("ffn_g", [128, DC], F32); bg = E.buf()
    cw = sb("ffn_cw", [128, 3, 2 * FC], F32); bcw = E.buf()
    cb = sb("ffn_cb", [128, 2 * FC], F32); bcb = E.buf()
    E.op("sp", lambda e: e.dma_start(out=g[:], in_=W["g"]), writes=[bg], dmasem="dc0")
    E.op("sp", lambda e: e.dma_start(out=cw[:], in_=W["cw"]), writes=[bcw], dmasem="dc1")
    E.op("sp", lambda e: e.dma_start(out=cb[:], in_=W["cb"]), writes=[bcb], dmasem="dc2")
    wdn = W["wdn"].rearrange("(k p) n -> p k n", p=128)
    wdring = Ring(E, nc, st, "wd", [128, FC, 128], BF16, 2)
    wup = W["wup"].rearrange("(c p) n -> p c n", p=128)
    wring = Ring(E, nc, st, "wu", [128, DC, GW], BF16, 6)
    hT = sb("ffn_h", [128, DC, ST], BF16); bh = [E.buf(), E.buf()]
    hH = sb("ffn_hh", [128, DC, HC], BF16); bhh = E.buf()
    gT = sb("ffn_gT", [128, FC, ST], BF16)
    bgT = [[E.buf() for _ in range(FC)] for _ in range(2)]
    rstd = sb("ffn_rstd", [128, TT], F32); brstd = E.buf()
    rstdh = sb("ffn_rstdh", [128, HC], F32); brstdh = E.buf()
    uh = sb("ffn_uh", [128, 2 * FC, HC], F32)
    buh = [E.buf() for _ in range(2 * FC)]
    uring = Ring(E, nc, st, "uext", [128, 2, HC + TT], F32, 3)
    yring = Ring(E, nc, st, "ycv", [128, TT], F32, 6)
    sring = Ring(E, nc, st, "ysl", [128, TT], F32, 3)
    if final_g is not None:
        fg = sb("fin_g", [128, DC], F32); bfg = E.buf()
        E.op("sp", lambda e: e.dma_start(out=fg[:], in_=final_g), writes=[bfg], dmasem="dc3")
        oring = sring

    def norm_cols(col0, ncols, xb, r, br, hdst, hoff, bhd):
        rms_stats(C, lambda c: xres[:, c, col0:col0 + ncols], xb, ncols, r, br)
        for c in range(DC):
            E.op("dve", lambda e, c=c: e.scalar_tensor_tensor(
                out=hdst[:, c, hoff:hoff + ncols], in0=xres[:, c, col0:col0 + ncols], scalar=g[:, c:c + 1],
                in1=r[:, :ncols], op0=ALU.mult, op1=ALU.mult),
                reads=xb + [bg, br], writes=[bhd])

    norm_cols(XH - HC, HC, [bxh], rstdh, brstdh, hH, 0, bhh)
    NS = T // ST
    for S in range(NS):
        for s in range(2):
            t = 2 * S + s
            norm_cols(XH + t * TT, TT, [bx[t]], rstd, brstd, hT, s * TT, bh[s])
        items = []
        groups = []
        for jg in range(FC * 128 // GW):
            grp = {"jg": jg, "wts": None}
            groups.append(grp)
            for jl in range(GW // 128):
                j = jg * (GW // 128) + jl
                for s in range(2):
                    items.append((j, jl, s, grp))
        LOOK = 2

        def load_group(gi):
            if gi >= len(groups) or groups[gi]["wts"] is not None:
                return
            wts = []
            for half in range(2):
                col = half * DFF + groups[gi]["jg"] * GW
                wt, bw, wi = wring.next()
                E.op("pool", lambda e, wt=wt, col=col: e.dma_start(out=wt[:], in_=wup[:, :, col:col + GW]),
                     writes=[bw], dmasem="dwu%d" % wi)
                wts.append((wt, bw, wi, col))
            groups[gi]["wts"] = wts

        state = {}

        def stageA(it):
            j, jl, s, grp = it
            t = 2 * S + s
            gi = groups.index(grp)
            for d_ in range(LOOK + 1):
                load_group(gi + d_)
            wts = grp["wts"]
            res = []
            pw, bpw, _ = C.pW.next()
            for half in range(2):
                wt, bw, _, _ = wts[half]
                ps = pw[:, half * TT:(half + 1) * TT]
                for c in range(DC):
                    E.op("pe", lambda e, c=c, wt=wt, ps=ps: e.matmul(ps, lhsT=wt[:, c, jl * 128:(jl + 1) * 128],
                                                                     rhs=hT[:, c, s * TT:(s + 1) * TT],
                                                                     start=(c == 0), stop=(c == DC - 1)),
                         reads=[bw, bh[s]], writes=[bpw])
                psh = bpsh = None
                if t == 0:
                    psh, bpsh, _ = C.pS.next()
                    for c in range(DC):
                        E.op("pe", lambda e, c=c, wt=wt, psh=psh: e.matmul(psh[:, :HC], lhsT=wt[:, c, jl * 128:(jl + 1) * 128],
                                                                           rhs=hH[:, c, :], start=(c == 0), stop=(c == DC - 1)),
                             reads=[bw, bhh], writes=[bpsh])
                res.append((ps, bpw, psh, bpsh))
            state[(j, s)] = {"pw": (pw, bpw), "ps": res}

        def stageB(it):
            j, jl, s, grp = it
            t = 2 * S + s
            ys = []
            pw, bpw = state[(j, s)]["pw"]
            ue2, bue, _ = uring.next()
            if t == 0:
                for half in range(2):
                    ps, bps, psh, bpsh = state[(j, s)]["ps"][half]
                    E.op("dve", lambda e, psh=psh, half=half: e.tensor_copy(ue2[:, half, 0:HC], psh[:, :HC]),
                         reads=[bpsh], writes=[bue])
            else:
                E.op("pool", lambda e: e.tensor_copy(ue2[:, :, 0:HC], uh[:, j::FC, :]),
                     reads=[buh[j], buh[FC + j]], writes=[bue])
            E.op("act", lambda e: e.activation(ue2[:, :, HC:HC + TT], pw[:, :].rearrange("p (a n) -> p a n", a=2), AF.Copy),
                 reads=[bpw], writes=[bue])
            if t < NT - 1:
                E.op("pool", lambda e: e.tensor_copy(uh[:, j::FC, :], ue2[:, :, TT:TT + HC]),
                     reads=[bue], writes=[buh[j], buh[FC + j]])
            for half in range(2):
                jj = half * FC + j
                ps, bps, psh, bpsh = state[(j, s)]["ps"][half]
                y, by, _ = yring.next()
                E.op("act", lambda e, y=y, ps=ps, jj=jj: e.activation(y[:, :], ps, AF.Identity,
                                                                       bias=cb[:, jj:jj + 1], scale=cw[:, 2, jj:jj + 1]),
                     reads=[bps, bcw, bcb], writes=[by])
                E.op("dve", lambda e, y=y, jj=jj, half=half: e.scalar_tensor_tensor(
                    out=y[:, :], in0=ue2[:, half, 1:1 + TT], scalar=cw[:, 1, jj:jj + 1], in1=y[:, :],
                    op0=ALU.mult, op1=ALU.add), reads=[bue, bcw, by], writes=[by])
                E.op("dve", lambda e, y=y, jj=jj, half=half: e.scalar_tensor_tensor(
                    out=y[:, :], in0=ue2[:, half, 0:TT], scalar=cw[:, 0, jj:jj + 1], in1=y[:, :],
                    op0=ALU.mult, op1=ALU.add), reads=[bue, bcw, by], writes=[by])
                ys.append((y, by))
            state[(j, s)]["ys"] = ys

        def stageC(it):
            j, jl, s, grp = it
            ys = state[(j, s)]["ys"]
            sl, bsl, _ = sring.next()
            E.op("act", lambda e, sl=sl, y=ys[0][0]: e.activation(sl[:, :], y[:, :], AF.Silu),
                 reads=[ys[0][1]], writes=[bsl])
            E.op("pool", lambda e, sl=sl, y=ys[1][0]: e.tensor_tensor(gT[:, j, s * TT:(s + 1) * TT], sl[:, :], y[:, :], ALU.mult),
                 reads=[bsl, ys[1][1]], writes=[bgT[s][j]])
            del state[(j, s)]

        n_it = len(items)
        for k in range(n_it + 2):
            if k < n_it:
                stageA(items[k])
            if 0 <= k - 1 < n_it:
                stageB(items[k - 1])
            if 0 <= k - 2 < n_it:
                stageC(items[k - 2])
        wds = {}

        def load_wd(oc):
            if oc < DC and oc not in wds:
                wt, bw, wi = wdring.next()
                E.op("pool", lambda e, wt=wt, oc=oc: e.dma_start(out=wt[:], in_=wdn[:, :, oc * 128:(oc + 1) * 128]),
                     writes=[bw], dmasem="dwd%d" % wi)
                wds[oc] = (wt, bw)

        load_wd(0)
        for oc in range(DC):
            load_wd(oc + 1)
            wt, bw = wds[oc]
            for s in range(2):
                t = 2 * S + s
                c0 = XH + t * TT
                pw_, bps, _ = C.pW.next()
                ps = pw_[:, 0:TT]
                for k in range(FC):
                    E.op("pe", lambda e, k=k, ps=ps, wt=wt: e.matmul(ps, lhsT=wt[:, k, :],
                                                                     rhs=gT[:, k, s * TT:(s + 1) * TT], start=(k == 0), stop=(k == FC - 1)),
                         reads=[bw, bgT[s][k]], writes=[bps])
                E.op("dve", lambda e, ps=ps, oc=oc, c0=c0: e.tensor_tensor(xres[:, oc, c0:c0 + TT], xres[:, oc, c0:c0 + TT],
                                                                         ps, ALU.add),
                     reads=[bps, bx[t]], writes=[bx[t]])
        for s in range(2):
            t = 2 * S + s
            c0 = XH + t * TT
            if final_g is not None:
                rms_stats(C, lambda c: xres[:, c, c0:c0 + TT], [bx[t]], TT, rstd, brstd)
                for c in range(DC):
                    o, bo, oi = oring.next()
                    E.op("dve", lambda e, c=c, o=o: e.scalar_tensor_tensor(
                        out=o[:, :], in0=xres[:, c, c0:c0 + TT], scalar=fg[:, c:c + 1], in1=rstd[:, :],
                        op0=ALU.mult, op1=ALU.mult), reads=[bx[t], bfg, brstd], writes=[bo])
                    E.op("sp", lambda e, c=c, o=o: e.dma_start(out=outT[c * 128:(c + 1) * 128, t * TT:(t + 1) * TT], in_=o[:, :]),
                         reads=[bo], dmasem="dout%d" % oi)
    if final_g is not None:
        E.finish("sp", oring.b)


def load_xres(C, xT_dram, ncol_halo):
    E = C.E
    xres = C.gsb("xres", [128, DC, ncol_halo + T], F32)
    bxh = E.buf("xh")
    bx = [E.buf("x%d" % t) for t in range(NT)]
    src = xT_dram.rearrange("(c p) t -> p c t", p=128)
    if ncol_halo:
        E.op("sp", lambda e: e.dma_start(out=xres[:, :, 0:ncol_halo], in_=src[:, :, 0:ncol_halo]),
             writes=[bxh], dmasem="dxh")
    for t in range(NT):
        a = ncol_halo + t * TT
        E.op("sp", lambda e, a=a: e.dma_start(out=xres[:, :, a:a + TT], in_=src[:, :, a:a + TT]),
             writes=[bx[t]], dmasem="dx%d" % t)
    return xres, bx, bxh


def store_xres(C, xres, bx, ncol_halo, outT):
    E = C.E
    dst = outT.rearrange("(c p) t -> p c t", p=128)
    for t in range(NT):
        a = ncol_halo + t * TT
        E.op("sp", lambda e, a=a, t=t: e.dma_start(out=dst[:, :, t * TT:(t + 1) * TT], in_=xres[:, :, a:a + TT]),
             reads=[bx[t]], dmasem="dxo%d" % t)
    E.finish("sp", bx)


XH0 = 32
AW = 512
CK = 31


def mixer_phase(C, xres, XH, bx, bxh, W):
    nc, E, sb = C.nc, C.E, C.sb
    st = C.st
    HC = 32
    ld = lambda name, shape, src, key: _ld(C, name, shape, src, key)
    g, bg = ld("mx_g", [128, DC], W["g"], "dc0")
    b_u, bb_u = ld("mx_bu", [128, 4], W["b_u"], "dc1")
    b_ag, bb_ag = ld("mx_bag", [128, 8], W["b_ag"], "dc2")
    bvb, bbvb = ld("mx_bvb", [128, AW], W["b_v_bc"], "dc3")
    vg, bvg = ld("mx_vg", [128, 4], W["vln_g"], "dc0")
    vb, bvb2 = ld("mx_vb", [128, 4], W["vln_b"], "dc1")
    wsT, bwsT = ld("mx_wsT", [128, 8, 128], W["wsT"], "dc2")
    maskT, bmaskT = ld("mx_maskT", [128, 128], W["maskT"], "dc3")
    bsbc, bbsbc = ld("mx_bsbc", [128, 4, 128], W["bs_bc"], "dc0")
    cw, bcw = ld("mx_cw", [128, 4, CK], W["cw"], "dc1")
    cb, bcb = ld("mx_cb", [128, 4], W["cb"], "dc2")
    cg, bcg = ld("mx_cg", [128, 4], W["cln_g"], "dc3")
    cbt, bcbt = ld("mx_cbt", [128, 4], W["cln_b"], "dc0")
    flag, bflag = ld("mx_flag", [128, 1], W["flag"], "dc1")
    for b in (bg, bb_u, bb_ag, bbvb, bvg, bvb2, bwsT, bmaskT, bbsbc, bcw, bcb, bcg, bcbt, bflag):
        b.w = (b.w[0], E.cnt[b.w[0]])
    win = sb("mx_win", [128, DC, 2048], BF16)
    wsrc = W["w_in"].rearrange("(c p) n -> p c n", p=128)
    bwinb = [E.buf() for _ in range(4)]
    for b_ in (2, 3, 0, 1):
        E.op("pool", lambda e, b_=b_: e.dma_start(out=win[:, :, b_ * 512:(b_ + 1) * 512], in_=wsrc[:, :, b_ * 512:(b_ + 1) * 512]),
             writes=[bwinb[b_]], dmasem="dwi%d" % b_)
    wout = sb("mx_wout", [128, DC, D], BF16); bwout = E.buf()
    wosrc = W["w_out"].rearrange("(c p) n -> p c n", p=128)
    for c in range(DC):
        E.op("pool", lambda e, c=c: e.dma_start(out=wout[:, c, :], in_=wosrc[:, c, :]), writes=[bwout],
             dmasem="dwu0", group=True)
    wsm = sb("mx_wsm", [128, 8, 128], BF16); bwsm = E.buf()
    for h in range(8):
        E.op("dve", lambda e, h=h: e.tensor_tensor(wsm[:, h, :], wsT[:, h, :], maskT[:, :], ALU.mult),
             reads=[bwsT, bmaskT], writes=[bwsm])
    Cb = sb("mx_Cb", [128, 4, 128], F32); bCb = E.buf()
    for i in range(4):
        ps, bps, _ = C.pS.next()
        for hh in range(2):
            E.op("pe", lambda e, hh=hh, i=i, ps=ps: e.matmul(ps[hh * 64:(hh + 1) * 64, 0:128], lhsT=C.ones[:, 0:64],
                                                           rhs=wsm[:, 2 * i + hh, :], start=True, stop=True),
                 reads=[C.bones, bwsm], writes=[bps])
        E.op("dve", lambda e, i=i, ps=ps: e.scalar_tensor_tensor(out=Cb[:, i, :], in0=ps[:, 0:128], scalar=vb[:, i:i + 1],
                                                                 in1=bsbc[:, i, :], op0=ALU.mult, op1=ALU.add),
             reads=[bps, bvb2, bbsbc], writes=[bCb])

    hT = sb("mx_h", [128, DC, TT], BF16); bh = E.buf()
    hH = sb("mx_hh", [128, DC, HC], BF16); bhh = E.buf()
    rstd = sb("mx_rstd", [128, TT], F32); brstd = E.buf()
    rstdh = sb("mx_rstdh", [128, HC], F32); brstdh = E.buf()
    uT = sb("mx_uT", [128, 4, TT], F32); buT = [E.buf() for _ in range(4)]
    yaT = sb("mx_yaT", [128, 4, TT], BF16); byaT = E.buf()
    ybT = sb("mx_ybT", [128, 4, TT], BF16); bybT = E.buf()
    hgr = Ring(E, nc, st, "mx_hg", [128, 4, HC + TT], F32, 1)
    ycv = sb("mx_ycv", [128, 4, TT], F32); bycv = [E.buf() for _ in range(4)]
    zring = Ring(E, nc, st, "mx_z", [128, AW], F32, 1)
    vring = Ring(E, nc, st, "mx_v", [128, AW], F32, 2)
    vnring = Ring(E, nc, st, "mx_vn", [128, AW], BF16, 2)
    sgring = Ring(E, nc, st, "mx_sg", [128, TT], F32, 2)
    tmpr = Ring(E, nc, st, "mx_tmp", [128, 128], F32, 3)
    stt = Ring(E, nc, st, "mx_st", [128, 8], F32, 3)
    tmpcv = sb("mx_tmpcv", [128, TT], F32); bptmp = E.buf()
    ybf = Ring(E, nc, st, "mx_ybf", [128, TT], BF16, 3)
    lnm = sb("mx_lnm", [128, TT], F32); blnm = E.buf()
    lnr = sb("mx_lnr", [128, TT], F32); blnr = E.buf()
    lnt = Ring(E, nc, st, "mx_lnt", [128, TT], F32, 2)

    def norm_cols(col0, ncols, xb, r, br, hdst, bhd):
        rms_stats(C, lambda c: xres[:, c, col0:col0 + ncols], xb, ncols, r, br)
        for c in range(DC):
            E.op("dve", lambda e, c=c: e.scalar_tensor_tensor(
                out=hdst[:, c, :ncols], in0=xres[:, c, col0:col0 + ncols], scalar=g[:, c:c + 1],
                in1=r[:, :ncols], op0=ALU.mult, op1=ALU.mult),
                reads=xb + [bg, br], writes=[bhd])

    def glu(hsrc, bhs, ncols, hg, bhg, dcol0, useflag):
        for j in range(4):
            pa, bpa, _ = C.pA.next()
            pg, bpg, _ = C.pB.next()
            for c in range(DC):
                E.op("pe", lambda e, c=c, pa=pa, j=j: e.matmul(pa[:, :ncols], lhsT=win[:, c, 1024 + j * 128:1024 + (j + 1) * 128],
                                                              rhs=hsrc[:, c, :ncols], start=(c == 0), stop=(c == DC - 1)),
                     reads=[bwinb[2], bhs], writes=[bpa])
            for c in range(DC):
                E.op("pe", lambda e, c=c, pg=pg, j=j: e.matmul(pg[:, :ncols], lhsT=win[:, c, 1536 + j * 128:1536 + (j + 1) * 128],
                                                              rhs=hsrc[:, c, :ncols], start=(c == 0), stop=(c == DC - 1)),
                     reads=[bwinb[3], bhs], writes=[bpg])
            sg, bsg, _ = sgring.next()
            E.op("act", lambda e, sg=sg, pg=pg, j=j: e.activation(sg[:, :ncols], pg[:, :ncols], AF.Sigmoid,
                                                                   bias=b_ag[:, 4 + j:5 + j]),
                 reads=[bpg, bb_ag], writes=[bsg])
            E.op("dve", lambda e, sg=sg, pa=pa, j=j: e.scalar_tensor_tensor(
                out=hg[:, j, dcol0:dcol0 + ncols], in0=pa[:, :ncols], scalar=b_ag[:, j:j + 1], in1=sg[:, :ncols],
                op0=ALU.add, op1=ALU.mult), reads=[bpa, bsg, bb_ag], writes=[bhg])
            if useflag:
                E.op("dve", lambda e, j=j: e.tensor_scalar(hg[:, j, dcol0:dcol0 + ncols], hg[:, j, dcol0:dcol0 + ncols],
                                                           flag[:, 0:1], None, ALU.mult),
                     reads=[bflag, bhg], writes=[bhg])

    norm_cols(XH - HC, HC, [bxh], rstdh, brstdh, hH, bhh)
    hg_prev = None
    for t in range(NT):
        c0 = XH + t * TT
        norm_cols(c0, TT, [bx[t]], rstd, brstd, hT, bh)
        E.defer = []
        for j in range(4):
            ps, bps, _ = C.pA.next()
            for c in range(DC):
                E.op("pe", lambda e, c=c, ps=ps, j=j: e.matmul(ps[:, :], lhsT=win[:, c, j * 128:(j + 1) * 128], rhs=hT[:, c, :],
                                                              start=(c == 0), stop=(c == DC - 1)),
                     reads=[bwinb[0], bh], writes=[bps])
            E.op("act", lambda e, ps=ps, j=j: e.activation(uT[:, j, :], ps[:, :], AF.Gelu_apprx_tanh, bias=b_u[:, j:j + 1]),
                 reads=[bps, bb_u], writes=[buT[j]])
        for blk in range(TT // 128):
            ps, bps, _ = C.pB.next()
            for c in range(DC):
                E.op("pe", lambda e, c=c, ps=ps, blk=blk: e.matmul(ps[:, :], lhsT=hT[:, c, blk * 128:(blk + 1) * 128],
                                                                  rhs=win[:, c, 512:1024], start=(c == 0), stop=(c == DC - 1)),
                     reads=[bwinb[1], bh], writes=[bps])
            z, bz, _ = zring.next()
            E.op("dve", lambda e, z=z, ps=ps: e.tensor_tensor(z[:, :], ps[:, :], bvb[:, :], ALU.add),
                 reads=[bps, bbvb], writes=[bz])
            v, bv, _ = vring.next()
            E.op("act", lambda e, z=z, v=v: e.activation(v[:, :], z[:, :], AF.Gelu_apprx_tanh), reads=[bz], writes=[bv])
            s6, bs6, _ = stt.next()
            E.op("dve", lambda e, v=v, s6=s6: e.bn_stats(s6[:, 0:6], v[:, :]), reads=[bv], writes=[bs6])
            m2, bm2, _ = stt.next()
            E.op("dve", lambda e, m2=m2, s6=s6: e.bn_aggr(m2[:, 0:2], s6[:, 0:6]), reads=[bs6], writes=[bm2])
            E.op("act", lambda e, m2=m2: e.activation(m2[:, 2:3], m2[:, 1:2], AF.Sqrt, bias=C.epsb[:, 0:1]),
                 reads=[bm2, C.bepsb], writes=[bm2])
            E.op("dve", lambda e, m2=m2: e.reciprocal(m2[:, 3:4], m2[:, 2:3]), reads=[bm2], writes=[bm2])
            vn, bvn, _ = vnring.next()
            E.op("dve", lambda e, v=v, vn=vn, m2=m2: e.tensor_scalar(vn[:, :], v[:, :], m2[:, 0:1], m2[:, 3:4],
                                                                      ALU.subtract, ALU.mult),
                 reads=[bv, bm2], writes=[bvn])
            for i in range(4):
                pm, bpm, _ = C.pS.next()
                for hh in range(2):
                    h = 2 * i + hh
                    E.op("pe", lambda e, hh=hh, h=h, pm=pm, vn=vn: e.matmul(pm[hh * 64:(hh + 1) * 64, 0:128],
                                                                           lhsT=vn[:, h * 64:(h + 1) * 64], rhs=wsm[:, h, :],
                                                                           start=True, stop=True),
                         reads=[bvn, bwsm], writes=[bpm])
                tm, btm, _ = tmpr.next()
                E.op("dve", lambda e, tm=tm, pm=pm, i=i: e.scalar_tensor_tensor(
                    out=tm[:, :], in0=pm[:, 0:128], scalar=vg[:, i:i + 1], in1=Cb[:, i, :], op0=ALU.mult, op1=ALU.add),
                    reads=[bpm, bvg, bCb], writes=[btm])
                E.op("pool", lambda e, tm=tm, i=i, blk=blk: e.tensor_tensor(yaT[:, i, blk * 128:(blk + 1) * 128], tm[:, :],
                                                                          uT[:, i, blk * 128:(blk + 1) * 128], ALU.mult),
                     reads=[btm, buT[i]], writes=[byaT])
        uv_ops = E.defer
        E.defer = None
        hg, bhg, _ = hgr.next()
        if t == 0:
            glu(hH, bhh, HC, hg, bhg, 0, True)
        else:
            E.op("pool", lambda e, hg=hg, hp=hg_prev[0]: e.tensor_copy(hg[:, :, 0:HC], hp[:, :, TT:TT + HC]),
                 reads=[bhg], writes=[bhg])
        glu(hT, bh, TT, hg, bhg, HC, False)
        hg_prev = (hg, bhg)
        ptmp = tmpcv
        for j in range(4):
            E.op("dve", lambda e, j=j, hg=hg: e.tensor_scalar(ycv[:, j, :], hg[:, j, 2:2 + TT], cw[:, j, 0:1], cb[:, j:j + 1],
                                                            ALU.mult, ALU.add),
                 reads=[bhg, bcw, bcb], writes=[bycv[j]])
        for k in range(1, CK):
            for j in range(3):
                E.op("dve", lambda e, j=j, k=k, hg=hg: e.scalar_tensor_tensor(
                    out=ycv[:, j, :], in0=hg[:, j, 2 + k:2 + k + TT], scalar=cw[:, j, k:k + 1], in1=ycv[:, j, :],
                    op0=ALU.mult, op1=ALU.add), reads=[bhg, bcw, bycv[j]], writes=[bycv[j]])
                E.replay(uv_ops, 2)
            E.op("pool", lambda e, k=k, hg=hg: e.tensor_scalar(ptmp[:, :], hg[:, 3, 2 + k:2 + k + TT], cw[:, 3, k:k + 1], 0.0,
                                                              ALU.mult, ALU.add),
                 reads=[bhg, bcw], writes=[bptmp])
            E.op("pool", lambda e: e.tensor_tensor(ycv[:, 3, :], ycv[:, 3, :], ptmp[:, :], ALU.add),
                 reads=[bptmp, bycv[3]], writes=[bycv[3]])
        E.replay(uv_ops, len(uv_ops))
        p1, bp1, _ = C.pS.next()
        p2, bp2, _ = C.pS.next()
        for j in range(4):
            yb1, byb1, _ = ybf.next()
            E.op("act", lambda e, j=j, yb1=yb1: e.activation(yb1[:, :], ycv[:, j, :], AF.Copy), reads=[bycv[j]], writes=[byb1])
            E.op("pe", lambda e, j=j, yb1=yb1: e.matmul(p1[:, :], lhsT=C.ones[:], rhs=yb1[:, :], start=(j == 0), stop=(j == 3)),
                 reads=[byb1, C.bones], writes=[bp1])
            yb2, byb2, _ = ybf.next()
            E.op("act", lambda e, j=j, yb2=yb2: e.activation(yb2[:, :], ycv[:, j, :], AF.Square), reads=[bycv[j]], writes=[byb2])
            E.op("pe", lambda e, j=j, yb2=yb2: e.matmul(p2[:, :], lhsT=C.ones[:], rhs=yb2[:, :], start=(j == 0), stop=(j == 3)),
                 reads=[byb2, C.bones], writes=[bp2])
        E.op("act", lambda e: e.activation(lnm[:, :], p1[:, :], AF.Copy, scale=1.0 / AW), reads=[bp1], writes=[blnm])
        lt, blt, _ = lnt.next()
        E.op("act", lambda e, lt=lt: e.activation(lt[:, :], p1[:, :], AF.Square, scale=1.0 / AW), reads=[bp1], writes=[blt])
        E.op("dve", lambda e, lt=lt: e.scalar_tensor_tensor(out=lnr[:, :], in0=p2[:, :], scalar=1.0 / AW, in1=lt[:, :],
                                                            op0=ALU.mult, op1=ALU.subtract),
             reads=[bp2, blt], writes=[blnr])
        E.op("act", lambda e: e.activation(lnr[:, :], lnr[:, :], AF.Sqrt, bias=C.epsb[:, 0:1]), reads=[blnr, C.bepsb], writes=[blnr])
        E.op("dve", lambda e: e.reciprocal(lnr[:, :], lnr[:, :]), reads=[blnr], writes=[blnr])
        for j in range(4):
            lt, blt, _ = lnt.next()
            E.op("dve", lambda e, j=j, lt=lt: e.tensor_tensor(lt[:, :], ycv[:, j, :], lnm[:, :], ALU.subtract),
                 reads=[bycv[j], blnm], writes=[blt])
            E.op("pool", lambda e, lt=lt: e.tensor_tensor(lt[:, :], lt[:, :], lnr[:, :], ALU.mult),
                 reads=[blt, blnr], writes=[blt])
            E.op("act", lambda e, j=j, lt=lt: e.activation(ybT[:, j, :], lt[:, :], AF.Silu, bias=cbt[:, j:j + 1], scale=cg[:, j:j + 1]),
                 reads=[blt, bcg, bcbt], writes=[bybT])
        for oc in range(DC):
            ps, bps, _ = C.pA.next()
            for k in range(DC):
                src, bsrc = (yaT, byaT) if k < 4 else (ybT, bybT)
                E.op("pe", lambda e, k=k, ps=ps, oc=oc, src=src: e.matmul(ps[:, :], lhsT=wout[:, k, oc * 128:(oc + 1) * 128],
                                                                         rhs=src[:, k % 4, :], start=(k == 0), stop=(k == DC - 1)),
                     reads=[bwout, bsrc], writes=[bps])
            E.op("dve", lambda e, ps=ps, oc=oc: e.tensor_tensor(xres[:, oc, c0:c0 + TT], xres[:, oc, c0:c0 + TT], ps[:, :], ALU.add),
                 reads=[bps, bx[t]], writes=[bx[t]])


def _ld(C, name, shape, src, key):
    t = C.sb(name, shape, F32)
    b = C.E.buf(name)
    C.E.op("sp", lambda e: e.dma_start(out=t[:], in_=src), writes=[b], dmasem=key)
    return t, b


def build_mixer():
    nc = bass.Bass("TRN2", target_bir_lowering=False)
    dt = lambda n, s, k="ExternalInput": nc.dram_tensor(n, s, F32, kind=k).ap()
    xT = dt("xT", [D, XH0 + T])
    W = {"g": dt("g", [128, DC]), "b_u": dt("b_u", [128, 4]), "b_ag": dt("b_ag", [128, 8]), "b_v_bc": dt("b_v_bc", [128, AW]),
         "vln_g": dt("vln_g", [128, 4]), "vln_b": dt("vln_b", [128, 4]), "wsT": dt("wsT", [128, 8, 128]),
         "maskT": dt("maskT", [128, 128]), "bs_bc": dt("bs_bc", [128, 4, 128]), "cw": dt("cw", [128, 4, CK]),
         "cb": dt("cb", [128, 4]), "cln_g": dt("cln_g", [128, 4]), "cln_b": dt("cln_b", [128, 4]), "flag": dt("flag", [128, 1]),
         "w_in": dt("w_in", [D, 2048]), "w_out": dt("w_out", [D, D])}
    outT = dt("outT", [D, T], "ExternalOutput")
    with ExitStack() as st:
        C = make_ctx(nc, st)
        xres, bx, bxh = load_xres(C, xT, XH0)
        phase_begin(C)
        mixer_phase(C, xres, XH0, bx, bxh, W)
        phase_end(C)
        store_xres(C, xres, bx, XH0, outT)
    return nc


def mixer_consts(inp):
    b_in = inp["even_b_in"][0]
    m = {"g": pvec(inp["even_norm_g"][0]), "b_u": pvec(b_in[0:512]), "b_ag": pvec(b_in[1024:2048]),
         "b_v_bc": np.ascontiguousarray(np.broadcast_to(b_in[512:1024][None, :], (128, AW))),
         "vln_g": pvec(inp["even_v_ln_g"][0]), "vln_b": pvec(inp["even_v_ln_b"][0]),
         "wsT": np.ascontiguousarray(inp["even_w_s"][0].transpose(2, 0, 1)),
         "maskT": np.triu(np.ones((128, 128), np.float32)),
         "bs_bc": np.ascontiguousarray(np.repeat(inp["even_b_s"][0], 64, axis=0).reshape(4, 128, 128).transpose(1, 0, 2)),
         "cw": np.ascontiguousarray(inp["even_conv_w"][0].reshape(CK, 4, 128).transpose(2, 1, 0)),
         "cb": pvec(inp["even_conv_b"][0]), "cln_g": pvec(inp["even_conv_ln_g"][0]), "cln_b": pvec(inp["even_conv_ln_b"][0]),
         "w_in": np.ascontiguousarray(inp["even_w_in"][0]), "w_out": np.ascontiguousarray(inp["even_w_out"][0])}
    return m


def mixer_host_inputs(x, inp):
    base = mixer_consts(inp)
    maps = []
    for core in range(NCORES):
        b, h = divmod(core, 2)
        xs = x[b, h * T:(h + 1) * T]
        halo = x[b, T - XH0:T] if h == 1 else np.zeros((XH0, D), np.float32)
        m = dict(base)
        m["xT"] = np.ascontiguousarray(np.concatenate([halo, xs], axis=0).T)
        m["flag"] = np.full((128, 1), float(h), np.float32)
        maps.append(m)
    return maps


def run_mixer(x, inp):
    if "mixer" not in _NC_CACHE:
        _NC_CACHE["mixer"] = build_mixer()
    res = run_bass_kernel_spmd(_NC_CACHE["mixer"], mixer_host_inputs(x, inp), core_ids=list(range(NCORES)))
    return gather_T(res)


import os
ATT_STAGE = int(os.environ.get('ATT_STAGE', '9'))
ATT_SUB = int(os.environ.get('ATT_SUB', '9'))
NH2 = 8
TX = 2 * T


def qkv_phase(C, xres, XH, bx, W, qT, bqT, ksnd, vsnd, bks, bvs):
    nc, E, sb, st = C.nc, C.E, C.sb, C.st
    g, bg = _ld(C, "qk_g", [128, DC], W["g"], "dc0")
    ctab, bct = _ld(C, "qk_ct", [128, T], W["ctab"], "dc1")
    stab, bst = _ld(C, "qk_st", [128, T], W["stab"], "dc2")
    perm, bpm = _ld(C, "qk_pm", [128, 128], W["perm"], "dc3")
    wq = sb("qk_w", [128, DC, 3 * D], BF16)
    wsrc = W["wqkv"].rearrange("(c p) n -> p c n", p=128)
    bwqb = [E.buf() for _ in range(6)]
    for b_ in range(6):
        E.op("pool", lambda e, b_=b_: e.dma_start(out=wq[:, :, b_ * 512:(b_ + 1) * 512], in_=wsrc[:, :, b_ * 512:(b_ + 1) * 512]),
             writes=[bwqb[b_]], dmasem="dwq%d" % b_)
    hT = sb("qk_h", [128, DC, TT], BF16); bh = E.buf()
    rstd = sb("qk_rstd", [128, TT], F32); brstd = E.buf()
    qfr = Ring(E, nc, st, "qk_qf", [128, TT], F32, 3)
    t1r = Ring(E, nc, st, "qk_t1", [128, TT], F32, 2)
    t2r = Ring(E, nc, st, "qk_t2", [128, TT], F32, 2)
    stg = Ring(E, nc, st, "qk_stg", [128, TT], BF16, 4)
    for t in range(NT):
        c0 = XH + t * TT
        tc0 = t * TT
        rms_stats(C, lambda c: xres[:, c, c0:c0 + TT], [bx[t]], TT, rstd, brstd)
        for c in range(DC):
            E.op("dve", lambda e, c=c: e.scalar_tensor_tensor(out=hT[:, c, :], in0=xres[:, c, c0:c0 + TT], scalar=g[:, c:c + 1],
                                                              in1=rstd[:, :], op0=ALU.mult, op1=ALU.mult),
                 reads=[bx[t], bg, brstd], writes=[bh])
        def partA(j):
            ps, bps, _ = C.pA.next()
            for c in range(DC):
                E.op("pe", lambda e, c=c, ps=ps, j=j: e.matmul(ps[:, :], lhsT=wq[:, c, j * 128:(j + 1) * 128], rhs=hT[:, c, :],
                                                              start=(c == 0), stop=(c == DC - 1)),
                     reads=[bwqb[j // 4], bh], writes=[bps])
            if j < 16:
                qf, bqf, _ = qfr.next()
                E.op("act", lambda e, qf=qf, ps=ps: e.activation(qf[:, :], ps[:, :], AF.Copy), reads=[bps], writes=[bqf])
                return (qf, bqf)
            sg, bsg, si = stg.next()
            E.op("act", lambda e, sg=sg, ps=ps: e.activation(sg[:, :], ps[:, :], AF.Copy), reads=[bps], writes=[bsg])
            E.op("sp", lambda e, sg=sg, j=j: e.dma_start(out=vsnd(j - 16)[:, tc0:tc0 + TT], in_=sg[:, :]),
                 reads=[bsg], writes=[bvs], dmasem="dst%d" % si, group=True)
            return None

        def partB(j, st_):
            qf, bqf = st_
            pq, bpq, _ = C.pB.next()
            E.op("pe", lambda e, pq=pq, qf=qf: e.matmul(pq[:, :], lhsT=perm[:, :], rhs=qf[:, :], start=True, stop=True),
                 reads=[bpm, bqf], writes=[bpq])
            t1, bt1, _ = t1r.next()
            E.op("pool", lambda e, t1=t1, qf=qf: e.tensor_tensor(t1[:, :], qf[:, :], ctab[:, tc0:tc0 + TT], ALU.mult),
                 reads=[bqf, bct], writes=[bt1])
            t2, bt2, _ = t2r.next()
            E.op("dve", lambda e, t2=t2, pq=pq: e.tensor_tensor(t2[:, :], pq[:, :], stab[:, tc0:tc0 + TT], ALU.mult),
                 reads=[bpq, bst], writes=[bt2])
            if j < 8:
                E.op("dve", lambda e, t1=t1, t2=t2, j=j: e.tensor_tensor(qT[:, j, tc0:tc0 + TT], t1[:, :], t2[:, :], ALU.add),
                     reads=[bt1, bt2], writes=[bqT])
            else:
                sg, bsg, si = stg.next()
                E.op("dve", lambda e, t1=t1, t2=t2, sg=sg: e.tensor_tensor(sg[:, :], t1[:, :], t2[:, :], ALU.add),
                     reads=[bt1, bt2], writes=[bsg])
                E.op("sp", lambda e, sg=sg, j=j: e.dma_start(out=ksnd(j - 8)[:, tc0:tc0 + TT], in_=sg[:, :]),
                     reads=[bsg], writes=[bks], dmasem="dst%d" % si, group=True)

        pend = None
        for j in range(24):
            st_ = partA(j)
            if pend is not None:
                partB(*pend)
            pend = (j, st_) if st_ is not None else None
        if pend is not None:
            partB(*pend)
    evs = {}
    for i in range(4):
        k = "dst%d" % i
        if k in E.cnt:
            evs[k] = E.cnt[k]
    C.kv_store_events = evs


def wait_events(C, e, evs):
    C.E._wait(e, dict(evs))


def attn_phase(C, qT, bqT, kprev, kown, vprev, vown, kv_events, oT, boT, W, vaug_dram=None):
    nc, E, sb, st = C.nc, C.E, C.sb, C.st
    ps_t = lambda n, s, d: st.enter_context(nc.psum_tensor(C.pf + n, s, d))
    acc = [ps_t("acc%d" % i, [128, 512], F32) for i in range(4)]
    bacc = [E.buf("acc%d" % i) for i in range(4)]
    sring = Ring(E, nc, st, "at_S", [128, 512], F32, 2, psum=True)
    ptring = Ring(E, nc, st, "at_pt", [128, 4, 128], F32, 2, psum=True)
    mk, bmk = _ld(C, "at_mk", [128, 256], W["mask"], "dc0")
    flag, bflag = _ld(C, "at_flag", [128, 1], W["flag"], "dc1")
    idf, bidf = _ld(C, "at_id", [128, 128], W["ident"], "dc2")
    ident = sb("at_idb", [128, 128], BF16); bident = E.buf()
    E.op("dve", lambda e: e.tensor_copy(ident[:, :], idf[:, :]), reads=[bidf], writes=[bident])
    mnorm = sb("at_mn", [128, 256], BF16); bmn = E.buf()
    mfirst = sb("at_mf", [128, 256], BF16); bmf = E.buf()
    E.op("dve", lambda e: e.tensor_copy(mnorm[:, :], mk[:, :]), reads=[bmk], writes=[bmn])
    E.op("dve", lambda e: e.tensor_copy(mfirst[:, 128:256], mk[:, 128:256]), reads=[bmk], writes=[bmf])
    E.op("dve", lambda e: e.tensor_scalar(mfirst[:, 0:128], mk[:, 0:128], flag[:, 0:1], None, ALU.mult),
         reads=[bmk, bflag], writes=[bmf])
    kx = Ring(E, nc, st, "at_kx", [128, TX], BF16, 2)
    vx = Ring(E, nc, st, "at_vx", [128, TX], BF16, 2)
    NBLK = 69
    vaug = sb("at_vaug", [128, NBLK, 3, 64], BF16); bvaug = E.buf()
    NVB = (NBLK + 3) // 4
    bvb = [E.buf() for _ in range(NVB)]
    if vaug_dram is None:
        E.op("pool", lambda e: e.memset(vaug[:, :, 1, :], 1.0), writes=[bvaug] + bvb)
    pr = Ring(E, nc, st, "at_P", [128, 256], BF16, 4)
    rl = sb("at_rl", [128, T], F32); brl = [E.buf() for _ in range(4)]

    slots = {}

    def cols(kind, a, r):
        if kind == 0:
            return slice(128 * a, 128 * a + 128)
        if kind == 1:
            return slice(512 * a + r, 512 * a + r + 4 * 127 + 1, 4)
        return slice(2048 * a + r, 2048 * a + r + 16 * 127 + 1, 16)

    blocks = []
    for a in range(15, 32):
        blocks.append((0, a, 0))
    for a in range(3, 8):
        for r in range(4):
            blocks.append((1, a, r))
    for a in range(2):
        for r in range(16):
            blocks.append((2, a, r))
    assert len(blocks) == NBLK
    for i, b in enumerate(blocks):
        slots[b] = i

    for c in range((NH2 if ATT_STAGE >= 9 else 1) if ATT_SUB >= 1 else 0):
        kt, bkt, ki = kx.next()
        vt, bvt, vi = vx.next()
        wait_events(C, "sp", kv_events(c) if callable(kv_events) else kv_events)
        E.op("sp", lambda e, kt=kt: e.dma_start(out=kt[:, 0:T], in_=kprev(c)), writes=[bkt], dmasem="dkx%d" % ki, group=True)
        E.op("sp", lambda e, kt=kt: e.dma_start(out=kt[:, T:TX], in_=kown(c)), writes=[bkt], dmasem="dkx%d" % ki, group=True)
        if vaug_dram is None:
            E.op("sp", lambda e, vt=vt: e.dma_start(out=vt[:, 0:T], in_=vprev(c)), writes=[bvt], dmasem="dvx%d" % vi, group=True)
            E.op("sp", lambda e, vt=vt: e.dma_start(out=vt[:, T:TX], in_=vown(c)), writes=[bvt], dmasem="dvx%d" % vi, group=True)
        else:
            E.op("sp", lambda e: e.dma_start(out=vaug[:, :, :, :].rearrange("p b a d -> p (b a d)"), in_=vaug_dram[c]), writes=[bvaug], dmasem="dvx0")
        nb_ = len(blocks) if vaug_dram is None else 0
        for bi, i0 in enumerate(range(0, nb_, 4)):
            n = min(4, nb_ - i0)
            pt, bpt, _ = ptring.next()
            for q_ in range(n):
                kind, a, r = blocks[i0 + q_]
                E.op("pe", lambda e, q_=q_, kind=kind, a=a, r=r: e.matmul(pt[:, q_, :], lhsT=vt[:, cols(kind, a, r)], rhs=ident[:, :],
                                                                       start=True, stop=True),
                     reads=[bvt, bident], writes=[bpt])
            src = pt[:, 0:n, :].rearrange("p n (a d) -> p n a d", a=2)
            if bi % 2 == 0:
                E.op("act", lambda e: e.activation(vaug[:, i0:i0 + n, 0:3:2, :], src, AF.Copy), reads=[bpt], writes=[bvb[bi]])
            else:
                E.op("dve", lambda e: e.tensor_copy(vaug[:, i0:i0 + n, 0:3:2, :], src), reads=[bpt], writes=[bvb[bi]])
        for hh in range(2 if ATT_STAGE >= 2 else 0):
            pp = slice(64 * hh, 64 * hh + 64)
            started = [False] * 4

            def lhs_v(slot):
                return vaug[:, slot, hh:hh + 2, :].rearrange("p a d -> p (a d)")

            units = []
            for qb in range(16):
                units.append((slice(128 * qb, 128 * qb + 128), (0, 15 + qb, 0), (0, 16 + qb, 0), qb == 0,
                              [(qb // 4, slice((qb % 4) * 128, (qb % 4) * 128 + 128), 0, 128)]))
            for a in range(4):
                for r in range(4):
                    units.append((slice(512 * a + r, 512 * a + r + 509, 4), (1, 3 + a, r), (1, 4 + a, r), a == 0,
                                  [(a, slice(r, r + 509, 4), 0, 128)]))
            for r in range(16):
                units.append((slice(r, r + 16 * 127 + 1, 16), (2, 0, r), (2, 1, r), True,
                              [(gq, slice(r, r + 16 * 31 + 1, 16), 32 * gq, 32) for gq in range(4)]))
            ust = {}

            def stA(i):
                qsl, kprev_b, kcur_b, first, outs = units[i]
                S, bS, _ = sring.next()
                E.op("pe", lambda e: e.matmul(S[:, 0:128], lhsT=kt[pp, cols(*kprev_b)], rhs=qT[pp, c, qsl], start=True, stop=True),
                     reads=[bkt, bqT], writes=[bS])
                E.op("pe", lambda e: e.matmul(S[:, 128:256], lhsT=kt[pp, cols(*kcur_b)], rhs=qT[pp, c, qsl], start=True, stop=True),
                     reads=[bkt, bqT], writes=[bS])
                ust[i] = (S, bS)

            def stB(i):
                qsl, kprev_b, kcur_b, first, outs = units[i]
                S, bS = ust[i]
                P, bP, _ = pr.next()
                E.op("act", lambda e: e.activation(P[:, :], S[:, 0:256], AF.Exp, scale=0.125), reads=[bS], writes=[bP])
                m, bm = (mfirst, bmf) if first else (mnorm, bmn)
                E.op("pool" if i % 2 == 0 else "dve", lambda e: e.tensor_tensor(P[:, :], P[:, :], m[:, :], ALU.mult), reads=[bP, bm], writes=[bP])
                ust[i] = (P, bP)

            def stC(i):
                qsl, kprev_b, kcur_b, first, outs = units[i]
                P, bP = ust.pop(i)
                for (bank, osl, po, n) in outs:
                    for part, kb in ((0, kprev_b), (1, kcur_b)):
                        stt = not started[bank]
                        started[bank] = True
                        E.op("pe", lambda e, bank=bank, osl=osl, po=po, n=n, part=part, kb=kb, stt=stt: e.matmul(
                            acc[bank][:, osl], lhsT=lhs_v(slots[kb]), rhs=P[:, part * 128 + po:part * 128 + po + n],
                            start=stt, stop=False, skip_group_check=True),
                            reads=[bP, bvaug, bvb[slots[kb] // 4]], writes=[bacc[bank]])

            nu = len(units) if ATT_STAGE >= 4 else 0
            for k in range(nu + 2):
                if k < nu:
                    stA(k)
                if 0 <= k - 1 < nu:
                    stB(k - 1)
                if 0 <= k - 2 < nu:
                    stC(k - 2)
            lp = slice(64 * (1 - hh), 64 * (1 - hh) + 64)
            if ATT_STAGE < 5:
                continue
            for gq in range(4):
                cs = slice(gq * 512, (gq + 1) * 512)
                E.op("act", lambda e, gq=gq, cs=cs: e.activation(rl[pp, cs], acc[gq][lp, :], AF.Ln),
                     reads=[bacc[gq]], writes=[brl[gq]])
                E.op("act", lambda e, cs=cs: e.activation(rl[pp, cs], rl[pp, cs], AF.Exp, scale=-1.0),
                     reads=[brl[gq]], writes=[brl[gq]])
                E.op("dve", lambda e, gq=gq, cs=cs: e.tensor_tensor(oT[pp, c, cs], acc[gq][pp, :], rl[pp, cs], ALU.mult),
                     reads=[bacc[gq], brl[gq]], writes=[boT])


def oproj_phase(C, xres, XH, bx, oT, boT, wo_dram):
    nc, E, sb = C.nc, C.E, C.sb
    wo = sb("op_w", [128, DC, D], BF16)
    wsrc = wo_dram.rearrange("(c p) n -> p c n", p=128)
    bwob = [E.buf() for _ in range(4)]
    for b_ in range(4):
        E.op("pool", lambda e, b_=b_: e.dma_start(out=wo[:, :, b_ * 256:(b_ + 1) * 256], in_=wsrc[:, :, b_ * 256:(b_ + 1) * 256]),
             writes=[bwob[b_]], dmasem="dwo%d" % b_)
    for t in range(NT):
        c0 = XH + t * TT
        for oc in range(DC):
            ps, bps, _ = C.pA.next()
            for k in range(DC):
                E.op("pe", lambda e, k=k, ps=ps, oc=oc: e.matmul(ps[:, :], lhsT=wo[:, k, oc * 128:(oc + 1) * 128],
                                                                 rhs=oT[:, k, t * TT:(t + 1) * TT], start=(k == 0), stop=(k == DC - 1)),
                     reads=[bwob[oc // 2], boT], writes=[bps])
            E.op("dve", lambda e, ps=ps, oc=oc: e.tensor_tensor(xres[:, oc, c0:c0 + TT], xres[:, oc, c0:c0 + TT], ps[:, :], ALU.add),
                 reads=[bps, bx[t]], writes=[bx[t]])


def v_blocks_host(v_ext):
    blocks = [(0, a, 0) for a in range(15, 32)] + [(1, a, r) for a in range(3, 8) for r in range(4)] + \
             [(2, a, r) for a in range(2) for r in range(16)]
    idx = []
    for kind, a, r in blocks:
        if kind == 0:
            idx.append(np.arange(128 * a, 128 * a + 128))
        elif kind == 1:
            idx.append(512 * a + r + 4 * np.arange(128))
        else:
            idx.append(2048 * a + r + 16 * np.arange(128))
    idx = np.stack(idx)
    vb = v_ext[idx]
    vb = vb.reshape(69, 128, 8, 2, 64)
    out = np.ones((8, 128, 69, 3, 64), dtype=v_ext.dtype)
    out[:, :, :, 0, :] = vb[:, :, :, 0, :].transpose(2, 1, 0, 3)
    out[:, :, :, 2, :] = vb[:, :, :, 1, :].transpose(2, 1, 0, 3)
    return np.ascontiguousarray(out.reshape(8, 128, 69 * 192))


def attn_consts(half):
    j = np.arange(32, dtype=np.float32)
    inv = (np.float32(10000.0) ** (-j / np.float32(32))).astype(np.float32)
    pos = (half * T + np.arange(T)).astype(np.float32)
    ang = pos[None, :] * inv[:, None]
    cos = np.cos(ang).astype(np.float32)
    sin = np.sin(ang).astype(np.float32)
    ctab = np.concatenate([cos, cos, cos, cos], axis=0)
    stab = np.concatenate([-sin, sin, -sin, sin], axis=0)
    perm = np.zeros((128, 128), np.float32)
    for m in range(128):
        d = m % 64
        k = m - d + ((d + 32) % 64)
        perm[k, m] = 1.0
    kk = np.arange(128)[:, None]
    qq = np.arange(128)[None, :]
    mask = np.concatenate([(kk >= qq), (kk <= qq)], axis=1).astype(np.float32)
    return {"ctab": np.ascontiguousarray(ctab), "stab": np.ascontiguousarray(stab), "perm": perm, "mask": mask,
            "ident": np.eye(128, dtype=np.float32), "flag": np.full((128, 1), float(half), np.float32)}


def dram_in(nc, n, s, dt=F32):
    return nc.dram_tensor(n, s, dt, kind="ExternalInput").ap()


def dram_out(nc, n, s, dt=F32):
    return nc.dram_tensor(n, s, dt, kind="ExternalOutput").ap()


def rowfn(ap):
    return lambda c: ap[c * 128:(c + 1) * 128, :]


def attn_weight_aps(nc):
    return {"g": dram_in(nc, "a_g", [128, DC]), "ctab": dram_in(nc, "a_ctab", [128, T]), "stab": dram_in(nc, "a_stab", [128, T]),
            "perm": dram_in(nc, "a_perm", [128, 128]), "wqkv": dram_in(nc, "a_wqkv", [D, 3 * D]),
            "mask": dram_in(nc, "a_mask", [128, 256]),
            "ident": dram_in(nc, "a_ident", [128, 128]), "wo": dram_in(nc, "a_wo", [D, D])}


def attn_host_consts(inp, half):
    c = attn_consts(half)
    return {"a_g": pvec(inp["odd_norm_g"][0]), "a_ctab": c["ctab"], "a_stab": c["stab"], "a_perm": c["perm"],
            "a_wqkv": np.ascontiguousarray(inp["odd_w_qkv"][0]), "a_mask": c["mask"], "a_flag": c["flag"],
            "a_ident": c["ident"], "a_wo": np.ascontiguousarray(inp["odd_w_o"][0])}


def build_attn_test():
    nc = bass.Bass("TRN2", target_bir_lowering=False)
    xT = dram_in(nc, "xT", [D, 2 + T])
    W = attn_weight_aps(nc)
    W["flag"] = dram_in(nc, "a_flag", [128, 1])
    kprev = dram_in(nc, "kprev", [D, T], BF16)
    vprev = dram_in(nc, "vprev", [D, T], BF16)
    ksnd = nc.dram_tensor("ksnd", [D, T], BF16).ap()
    vsnd = nc.dram_tensor("vsnd", [D, T], BF16).ap()
    vaug_d = dram_in(nc, "vaug", [NH2, 128, 69 * 192], BF16)
    outT = dram_out(nc, "outT", [D, T])
    with ExitStack() as st:
        C = make_ctx(nc, st)
        E = C.E
        xres, bx, bxh = load_xres(C, xT, 2)
        qT = C.gsb("qT", [128, NH2, T], BF16); bqT = E.buf("qT")
        bks = E.buf("ks"); bvs = E.buf("vs")
        phase_begin(C)
        qkv_phase(C, xres, 2, bx, W, qT, bqT, rowfn(ksnd), rowfn(vsnd), bks, bvs)
        phase_end(C)
        oT = C.gsb("oT", [128, NH2, T], BF16); boT = E.buf("oT")
        if ATT_STAGE >= 1:
            phase_begin(C, psum=False)
            attn_phase(C, qT, bqT, rowfn(kprev), rowfn(ksnd), rowfn(vprev), rowfn(vsnd), {}, oT, boT, W, vaug_dram=(vaug_d if os.environ.get('HOSTV') else None))
            phase_end(C)
        if ATT_STAGE >= 9:
            phase_begin(C)
            oproj_phase(C, xres, 2, bx, oT, boT, W["wo"])
            phase_end(C)
        store_xres(C, xres, bx, 2, outT)
    return nc


def build_ffn(final):
    nc = bass.Bass("TRN2", target_bir_lowering=False)
    dt = lambda n, s, k="ExternalInput": nc.dram_tensor(n, s, F32, kind=k).ap()
    xT = dt("xT", [D, 2 + T])
    W = {"g": dt("g", [128, DC]), "wup": dt("wup", [D, 2 * DFF]), "cw": dt("cw", [128, 3, 2 * FC]),
         "cb": dt("cb", [128, 2 * FC]), "wdn": dt("wdn", [DFF, D])}
    fg = dt("fg", [128, DC]) if final else None
    outT = dt("outT", [D, T], "ExternalOutput")
    with ExitStack() as st:
        C = make_ctx(nc, st)
        xres, bx, bxh = load_xres(C, xT, 2)
        phase_begin(C, psum="ffn")
        if final:
            ffn_phase(C, xres, 2, bx, bxh, W, final_g=fg, outT=outT)
            phase_end(C)
        else:
            ffn_phase(C, xres, 2, bx, bxh, W)
            phase_end(C)
            store_xres(C, xres, bx, 2, outT)
    return nc


def pvec(v):
    return np.ascontiguousarray(v.reshape(-1, 128).T)


def ffn_host_inputs(xmid, i, inp, final):
    maps = []
    cw = inp["ffn_conv_w"][i]
    cwl = np.ascontiguousarray(cw.reshape(3, 2 * FC, 128).transpose(2, 0, 1))
    cbl = pvec(inp["ffn_conv_b"][i])
    for core in range(NCORES):
        b, h = divmod(core, 2)
        xs = xmid[b, h * T:(h + 1) * T]
        halo = xmid[b, T - 2:T] if h == 1 else np.zeros((2, D), np.float32)
        xT = np.ascontiguousarray(np.concatenate([halo, xs], axis=0).T)
        m = {"xT": xT, "g": pvec(inp["ffn_norm_g"][i]), "wup": np.ascontiguousarray(inp["ffn_w_up"][i]),
             "cw": cwl, "cb": cbl, "wdn": np.ascontiguousarray(inp["ffn_w_down"][i])}
        if final:
            m["fg"] = pvec(inp["final_norm_g"])
        maps.append(m)
    return maps


def gather_T(res, key="outT"):
    out = np.empty((4, 2 * T, D), np.float32)
    for core in range(NCORES):
        b, h = divmod(core, 2)
        out[b, h * T:(h + 1) * T] = res.results[core][key].T
    return out


def run_ffn(xmid, i, inp, final):
    key = ("ffn", final)
    if key not in _NC_CACHE:
        _NC_CACHE[key] = build_ffn(final)
    res = run_bass_kernel_spmd(_NC_CACHE[key], ffn_host_inputs(xmid, i, inp, final), core_ids=list(range(NCORES)))
    return gather_T(res)

def build_qkv():
    nc = bass.Bass("TRN2", target_bir_lowering=False)
    xT = dram_in(nc, "xT", [D, 2 + T])
    W = attn_weight_aps(nc)
    qo = dram_out(nc, "qo", [D, T], BF16)
    ko = dram_out(nc, "ko", [D, T], BF16)
    vo = dram_out(nc, "vo", [D, T], BF16)
    with ExitStack() as st:
        C = make_ctx(nc, st)
        E = C.E
        xres, bx, bxh = load_xres(C, xT, 2)
        qT = C.gsb("qT", [128, NH2, T], BF16); bqT = E.buf("qT")
        bks = E.buf("ks"); bvs = E.buf("vs")
        phase_begin(C)
        qkv_phase(C, xres, 2, bx, W, qT, bqT, rowfn(ko), rowfn(vo), bks, bvs)
        phase_end(C)
        bq2 = E.buf()
        E.op("sp", lambda e: e.dma_start(out=qo.rearrange("(c p) t -> p c t", p=128), in_=qT[:, :, :]), reads=[bqT], writes=[bq2], dmasem="dxo0")
        E.finish("sp", [bq2, bks, bvs])
        barrier(C)
    return nc


def build_attn():
    nc = bass.Bass("TRN2", target_bir_lowering=False)
    xT = dram_in(nc, "xT", [D, 2 + T])
    W = attn_weight_aps(nc)
    qi = dram_in(nc, "qi", [D, T], BF16)
    kprev = dram_in(nc, "kprev", [D, T], BF16)
    kown = dram_in(nc, "kown", [D, T], BF16)
    vaug_d = dram_in(nc, "vaug", [NH2, 128, 69 * 192], BF16)
    outT = dram_out(nc, "outT", [D, T])
    with ExitStack() as st:
        C = make_ctx(nc, st)
        E = C.E
        xres, bx, bxh = load_xres(C, xT, 2)
        qT = C.gsb("qT", [128, NH2, T], BF16); bqT = E.buf("qT")
        E.op("sp", lambda e: e.dma_start(out=qT[:, :, :], in_=qi.rearrange("(c p) t -> p c t", p=128)), writes=[bqT], dmasem="dxh")
        oT = C.gsb("oT", [128, NH2, T], BF16); boT = E.buf("oT")
        phase_begin(C, psum=False)
        attn_phase(C, qT, bqT, rowfn(kprev), rowfn(kown), None, None, {}, oT, boT, W, vaug_dram=vaug_d)
        phase_end(C)
        phase_begin(C)
        oproj_phase(C, xres, 2, bx, oT, boT, W["wo"])
        phase_end(C)
        store_xres(C, xres, bx, 2, outT)
    return nc


def xT_maps(xfull, halo):
    out = []
    for core in range(NCORES):
        b, h = divmod(core, 2)
        xs = xfull[b, h * T:(h + 1) * T]
        hl = xfull[b, T - halo:T] if h == 1 else np.zeros((halo, D), np.float32)
        out.append(np.ascontiguousarray(np.concatenate([hl, xs], axis=0).T))
    return out


def run_qkv(x1, inp):
    if "qkv" not in _NC_CACHE:
        _NC_CACHE["qkv"] = build_qkv()
    xs = xT_maps(x1, 2)
    maps = []
    for core in range(NCORES):
        m = attn_host_consts(inp, core % 2)
        m["xT"] = xs[core]
        maps.append(m)
    res = run_bass_kernel_spmd(_NC_CACHE["qkv"], maps, core_ids=list(range(NCORES)))
    return [(r["qo"], r["ko"], r["vo"]) for r in res.results]


def run_attn(x1, qkv, inp):
    if "attn" not in _NC_CACHE:
        _NC_CACHE["attn"] = build_attn()
    xs = xT_maps(x1, 2)
    maps = []
    for core in range(NCORES):
        b, h = divmod(core, 2)
        m = attn_host_consts(inp, h)
        m["xT"] = xs[core]
        q, k, v = qkv[core]
        kp, vp = qkv[2 * b][1], qkv[2 * b][2]
        m["qi"] = q
        m["kown"] = k
        m["kprev"] = kp
        vext = np.concatenate([np.ascontiguousarray(vp.T), np.ascontiguousarray(v.T)], axis=0)
        m["vaug"] = v_blocks_host(vext)
        maps.append(m)
    res = run_bass_kernel_spmd(_NC_CACHE["attn"], maps, core_ids=list(range(NCORES)))
    return gather_T(res)


RG_PAIRS = [[0, 1], [2, 3], [4, 5], [6, 7]]
MIX_SHAPES = {"g": [128, DC], "b_u": [128, 4], "b_ag": [128, 8], "b_v_bc": [128, AW], "vln_g": [128, 4], "vln_b": [128, 4],
              "wsT": [128, 8, 128], "maskT": [128, 128], "bs_bc": [128, 4, 128], "cw": [128, 4, CK], "cb": [128, 4],
              "cln_g": [128, 4], "cln_b": [128, 4], "w_in": [D, 2048], "w_out": [D, D]}
FFN_SHAPES = {"g": [128, DC], "wup": [D, 2 * DFF], "cw": [128, 3, 2 * FC], "cb": [128, 2 * FC], "wdn": [DFF, D]}


def exchange_tail(C, xres, XH, bx_last, bxh, snd, rcv, flag, bflag, key):
    E = C.E
    bs = E.buf()
    br = E.buf()
    E.op("sp", lambda e: e.dma_start(out=snd.rearrange("(c p) t -> p c t", p=128), in_=xres[:, :, XH + T - 2:XH + T]),
         reads=[bx_last], writes=[bs], dmasem=key + "s")
    E.op("pool", lambda e: e.collective_compute("AllGather", ALU.bypass, replica_groups=RG_PAIRS, ins=[snd], outs=[rcv]),
         reads=[bs], writes=[br], dmasem=key + "c", inc=1)
    E.op("sp", lambda e: e.dma_start(out=xres[:, :, XH - 2:XH], in_=rcv[0:D, :].rearrange("(c p) t -> p c t", p=128)),
         reads=[br], writes=[bxh], dmasem=key + "l")
    E.op("dve", lambda e: e.tensor_scalar(xres[:, :, XH - 2:XH], xres[:, :, XH - 2:XH], flag[:, 0:1], None, ALU.mult),
         reads=[bflag, bxh], writes=[bxh])


def build_fused():
    nc = bass.Bass("TRN2", target_bir_lowering=False, num_devices=NCORES)
    xT = dram_in(nc, "xT", [D, XH0 + T])
    flag_d = dram_in(nc, "flag", [128, 1])
    Wm = {k: dram_in(nc, "m_" + k, sh) for k, sh in MIX_SHAPES.items()}
    Wm["flag"] = flag_d
    Wf = [{k: dram_in(nc, "f%d_%s" % (i, k), sh) for k, sh in FFN_SHAPES.items()} for i in range(2)]
    fg = dram_in(nc, "fg", [128, DC])
    Wa = attn_weight_aps(nc)
    Wa["flag"] = flag_d
    outT = dram_out(nc, "outT", [D, T])
    tls = [nc.dram_tensor("tl_s%d" % i, [D, 2], F32).ap() for i in range(2)]
    tlr = [nc.dram_tensor("tl_r%d" % i, [2 * D, 2], F32).ap() for i in range(2)]
    NX = 4
    ksnd = [nc.dram_tensor("ksnd%d" % i, [256, T], BF16).ap() for i in range(NX)]
    vsnd = [nc.dram_tensor("vsnd%d" % i, [256, T], BF16).ap() for i in range(NX)]
    kall = [nc.dram_tensor("kall%d" % i, [512, T], BF16).ap() for i in range(NX)]
    vall = [nc.dram_tensor("vall%d" % i, [512, T], BF16).ap() for i in range(NX)]
    own = lambda lst: (lambda c: lst[c // 2][(c % 2) * 128:(c % 2) * 128 + 128, :])
    with ExitStack() as st:
        C = make_ctx(nc, st)
        E = C.E
        xres, bx, bxh = load_xres(C, xT, XH0)
        flag = C.gsb("flag_sb", [128, 1], F32)
        bflag = E.buf("flag")
        E.op("sp", lambda e: e.dma_start(out=flag[:], in_=flag_d), writes=[bflag], dmasem="dflag")
        phase_begin(C)
        mixer_phase(C, xres, XH0, bx, bxh, Wm)
        phase_end(C)
        exchange_tail(C, xres, XH0, bx[NT - 1], bxh, tls[0], tlr[0], flag, bflag, "e1")
        phase_begin(C, psum="ffn")
        ffn_phase(C, xres, XH0, bx, bxh, Wf[0])
        phase_end(C)
        with ExitStack() as ast:
            qT = ast.enter_context(nc.sbuf_tensor("qT", [128, NH2, T], BF16))
            bqT = E.buf("qT")
            bks = E.buf("ks")
            bvs = E.buf("vs")
            phase_begin(C)
            qkv_phase(C, xres, XH0, bx, Wa, qT, bqT, own(ksnd), own(vsnd), bks, bvs)
            phase_end(C)
            bka = E.buf()
            bva = E.buf()
            for i in range(NX):
                E.op("pool", lambda e, i=i: e.collective_compute("AllGather", ALU.bypass, replica_groups=RG_PAIRS, ins=[ksnd[i]], outs=[kall[i]]),
                     writes=[bka], dmasem="cck", inc=1, group=True)
                E.op("pool", lambda e, i=i: e.collective_compute("AllGather", ALU.bypass, replica_groups=RG_PAIRS, ins=[vsnd[i]], outs=[vall[i]]),
                     writes=[bva], dmasem="ccv", inc=1, group=True)
            kv_events = lambda c: {"cck": c // 2 + 1, "ccv": c // 2 + 1}
            oT = ast.enter_context(nc.sbuf_tensor("oT", [128, NH2, T], BF16))
            boT = E.buf("oT")
            phase_begin(C, psum=False)
            attn_phase(C, qT, bqT, own(kall), own(ksnd), own(vall), own(vsnd), kv_events, oT, boT, Wa)
            phase_end(C)
            phase_begin(C)
            oproj_phase(C, xres, XH0, bx, oT, boT, Wa["wo"])
            phase_end(C)
        exchange_tail(C, xres, XH0, bx[NT - 1], bxh, tls[1], tlr[1], flag, bflag, "e3")
        phase_begin(C, psum="ffn")
        ffn_phase(C, xres, XH0, bx, bxh, Wf[1], final_g=fg, outT=outT)
        phase_end(C)
    return nc


def ffn_consts(inp, i):
    cw = inp["ffn_conv_w"][i]
    return {"g": pvec(inp["ffn_norm_g"][i]), "wup": np.ascontiguousarray(inp["ffn_w_up"][i]),
            "cw": np.ascontiguousarray(cw.reshape(3, 2 * FC, 128).transpose(2, 0, 1)), "cb": pvec(inp["ffn_conv_b"][i]),
            "wdn": np.ascontiguousarray(inp["ffn_w_down"][i])}


def kernel(**inputs):
    inp = {k: np.asarray(v) for k, v in inputs.items()}
    x = inp["x"]
    if "fused" not in _NC_CACHE:
        _NC_CACHE["fused"] = build_fused()
    base = {}
    for k, v in mixer_consts(inp).items():
        base["m_" + k] = v
    for i in range(2):
        for k, v in ffn_consts(inp, i).items():
            base["f%d_%s" % (i, k)] = v
    base["fg"] = pvec(inp["final_norm_g"])
    xs = xT_maps(x, XH0)
    maps = []
    for core in range(NCORES):
        h = core % 2
        m = dict(base)
        ac = attn_host_consts(inp, h)
        ac.pop("a_flag")
        m.update(ac)
        m["xT"] = xs[core]
        m["flag"] = np.full((128, 1), float(h), np.float32)
        maps.append(m)
    res = run_bass_kernel_spmd(_NC_CACHE["fused"], maps, core_ids=list(range(NCORES)))
    return gather_T(res).astype(np.float32)
```

```python
from contextlib import ExitStack
import numpy as np
import ml_dtypes
import concourse.bass as bass
import concourse.mybir as mybir
from concourse.bass_utils import run_bass_kernel_spmd

F32 = mybir.dt.float32
BF16 = mybir.dt.bfloat16
AF = mybir.ActivationFunctionType
ALU = mybir.AluOpType

NCORES = 8
_NC_CACHE = {}
D = 1024
DC = 8
T = 2048
TT = 512
NT = T // TT
DFF = 2816
FC = DFF // 128
EPS = 1e-6


class Buf:
    __slots__ = ("name", "w", "r")

    def __init__(self, name):
        self.name = name
        self.w = None
        self.r = {}


class Emit:
    def __init__(self, nc, stack):
        self.nc = nc
        self.stack = stack
        self.eng = {"pe": nc.tensor, "act": nc.scalar, "dve": nc.vector,
                    "pool": nc.gpsimd, "sp": nc.sync}
        self.sem = {}
        self.cnt = {}
        self.seen = {k: {} for k in self.eng}
        for k in ("pe", "act", "dve", "pool"):
            self._mksem(k)
        self.nbuf = 0
        self.defer = None

    def _mksem(self, key):
        self.sem[key] = self.stack.enter_context(self.nc.semaphore("s_" + key))
        self.cnt[key] = 0

    def buf(self, name=None):
        self.nbuf += 1
        return Buf(name or "b%d" % self.nbuf)

    def _wait(self, e, need, skipkey=None):
        eng = self.eng[e]
        seen = self.seen[e]
        for k, c in need.items():
            if (e == "pe" and k == "pe") or k == skipkey:
                continue
            if seen.get(k, 0) < c:
                eng.wait_ge(self.sem[k], c)
                seen[k] = c

    def op(self, e, fn, reads=(), writes=(), dmasem=None, group=False, inc=16):
        if self.defer is not None:
            self.defer.append((e, fn, list(reads), list(writes), dmasem, group, inc))
            return None
        need = {}

        def add(ev):
            if ev is None:
                return
            k, c = ev
            if need.get(k, 0) < c:
                need[k] = c
        for b in reads:
            add(b.w)
        for b in writes:
            add(b.w)
            for k, c in b.r.items():
                add((k, c))
        self._wait(e, need, skipkey=(dmasem if group else None))
        ins = fn(self.eng[e])
        if dmasem is not None:
            if dmasem not in self.sem:
                self._mksem(dmasem)
            key, inc = dmasem, inc
        else:
            key, inc = e, 1
        ins.then_inc(self.sem[key], inc)
        self.cnt[key] += inc
        ev = (key, self.cnt[key])
        for b in reads:
            if b.r.get(key, 0) < ev[1]:
                b.r[key] = ev[1]
        for b in writes:
            b.w = ev
            b.r = {}
        return ins

    def replay(self, ops, n):
        while n > 0 and ops:
            a = ops.pop(0)
            self.op(a[0], a[1], reads=a[2], writes=a[3], dmasem=a[4], group=a[5], inc=a[6])
            n -= 1

    def finish(self, e, bufs):
        need = {}
        for b in bufs:
            if b.w is not None:
                k, c = b.w
                need[k] = max(need.get(k, 0), c)
            for k, c in b.r.items():
                need[k] = max(need.get(k, 0), c)
        self._wait(e, need)


class Ring:
    uid = 0

    def __init__(self, E, nc, st, name, shape, dtype, n, psum=False):
        self.t = []
        self.b = []
        Ring.uid += 1
        name = "%s_r%d_" % (name, Ring.uid)
        for i in range(n):
            if psum:
                t = st.enter_context(nc.psum_tensor("%s%d" % (name, i), shape, dtype))
            else:
                t = st.enter_context(nc.sbuf_tensor("%s%d" % (name, i), shape, dtype))
            self.t.append(t)
            self.b.append(E.buf("%s%d" % (name, i)))
        self.i = 0
        self.n = n

    def next(self):
        i = self.i
        self.i = (i + 1) % self.n
        return self.t[i], self.b[i], i


class Ctx:
    pass


def make_ctx(nc, st):
    C = Ctx()
    C.nc = nc
    C.gst = st
    C.st = st
    C.E = Emit(nc, st)
    C.pf = "g_"
    C.sb = lambda n, s, d: C.st.enter_context(nc.sbuf_tensor(C.pf + n, s, d))
    C.gsb = lambda n, s, d: C.gst.enter_context(nc.sbuf_tensor(n, s, d))
    C.nphase = 0
    return C


def phase_begin(C, psum=True):
    E, nc = C.E, C.nc
    C.nphase += 1
    C.st = ExitStack()
    C.st.__enter__()
    st = C.st
    pf = "p%d_" % C.nphase
    if psum == "ffn":
        C.pW = Ring(E, nc, st, pf + "pW", [128, 2 * TT], F32, 3, psum=True)
        C.pS = Ring(E, nc, st, pf + "pS", [128, 512], F32, 2, psum=True)
    elif psum:
        C.pA = Ring(E, nc, st, pf + "pA", [128, 512], F32, 3, psum=True)
        C.pB = Ring(E, nc, st, pf + "pB", [128, 512], F32, 3, psum=True)
        C.pS = Ring(E, nc, st, pf + "pS", [128, 512], F32, 2, psum=True)
    C.pf = pf
    C.ones = C.sb("ones", [128, 128], BF16)
    C.bones = E.buf("ones")
    E.op("pool", lambda e: e.memset(C.ones[:], 1.0), writes=[C.bones])
    C.epsb = C.sb("epsb", [128, 1], F32)
    C.bepsb = E.buf()
    E.op("pool", lambda e: e.memset(C.epsb[:], EPS), writes=[C.bepsb])
    C.sq = Ring(E, nc, st, pf + "sq", [128, TT], BF16, 3)
    C.pf = pf


def phase_end(C):
    barrier(C)
    C.st.__exit__(None, None, None)
    C.st = C.gst


def barrier(C):
    E = C.E
    need = {k: c for k, c in E.cnt.items() if c > 0}
    for e in E.eng:
        E._wait(e, dict(need))


def rms_stats(C, xsrc, xb, ncols, rstd, brstd):
    E = C.E
    ps, bps, _ = C.pS.next()
    for c in range(DC):
        sq, bsq, _ = C.sq.next()
        E.op("act", lambda e, c=c, sq=sq: e.activation(sq[:, :ncols], xsrc(c), AF.Square),
             reads=xb, writes=[bsq])
        E.op("pe", lambda e, c=c, sq=sq: e.matmul(ps[:, :ncols], lhsT=C.ones[:], rhs=sq[:, :ncols],
                                                   start=(c == 0), stop=(c == DC - 1)),
             reads=[bsq, C.bones], writes=[bps])
    E.op("act", lambda e: e.activation(rstd[:, :ncols], ps[:, :ncols], AF.Sqrt,
                                        bias=C.epsb[:, 0:1], scale=1.0 / D),
         reads=[bps, C.bepsb], writes=[brstd])
    E.op("dve", lambda e: e.reciprocal(rstd[:, :ncols], rstd[:, :ncols]), reads=[brstd], writes=[brstd])


def ffn_phase(C, xres, XH, bx, bxh, W, final_g=None, outT=None):
    nc, st, E, sb = C.nc, C.st, C.E, C.sb
    HC = 2
    ST = 2 * TT
    GW = 256
    g = sb("ffn_g", [128, DC], F32); bg = E.buf()
    cw = sb("ffn_cw", [128, 3, 2 * FC], F32); bcw = E.buf()
    cb = sb("ffn_cb", [128, 2 * FC], F32); bcb = E.buf()
    E.op("sp", lambda e: e.dma_start(out=g[:], in_=W["g"]), writes=[bg], dmasem="dc0")
    E.op("sp", lambda e: e.dma_start(out=cw[:], in_=W["cw"]), writes=[bcw], dmasem="dc1")
    E.op("sp", lambda e: e.dma_start(out=cb[:], in_=W["cb"]), writes=[bcb], dmasem="dc2")
    wdn = W["wdn"].rearrange("(k p) n -> p k n", p=128)
    wdring = Ring(E, nc, st, "wd", [128, FC, 128], BF16, 2)
    wup = W["wup"].rearrange("(c p) n -> p c n", p=128)
    wring = Ring(E, nc, st, "wu", [128, DC, GW], BF16, 6)
    hT = sb("ffn_h", [128, DC, ST], BF16); bh = [E.buf(), E.buf()]
    hH = sb("ffn_hh", [128, DC, HC], BF16); bhh = E.buf()
    gT = sb("ffn_gT", [128, FC, ST], BF16)
    bgT = [[E.buf() for _ in range(FC)] for _ in range(2)]
    rstd = sb("ffn_rstd", [128, TT], F32); brstd = E.buf()
    rstdh = sb("ffn_rstdh", [128, HC], F32); brstdh = E.buf()
    uh = sb("ffn_uh", [128, 2 * FC, HC], F32)
    buh = [E.buf() for _ in range(2 * FC)]
    uring = Ring(E, nc, st, "uext", [128, 2, HC + TT], F32, 3)
    yring = Ring(E, nc, st, "ycv", [128, TT], F32, 6)
    sring = Ring(E, nc, st, "ysl", [128, TT], F32, 3)
    if final_g is not None:
        fg = sb("fin_g", [128, DC], F32); bfg = E.buf()
        E.op("sp", lambda e: e.dma_start(out=fg[:], in_=final_g), writes=[bfg], dmasem="dc3")
        oring = sring

    def norm_cols(col0, ncols, xb, r, br, hdst, hoff, bhd):
        rms_stats(C, lambda c: xres[:, c, col0:col0 + ncols], xb, ncols, r, br)
        for c in range(DC):
            E.op("dve", lambda e, c=c: e.scalar_tensor_tensor(
                out=hdst[:, c, hoff:hoff + ncols], in0=xres[:, c, col0:col0 + ncols], scalar=g[:, c:c + 1],
                in1=r[:, :ncols], op0=ALU.mult, op1=ALU.mult),
                reads=xb + [bg, br], writes=[bhd])

    norm_cols(XH - HC, HC, [bxh], rstdh, brstdh, hH, 0, bhh)
    NS = T // ST
    for S in range(NS):
        for s in range(2):
            t = 2 * S + s
            norm_cols(XH + t * TT, TT, [bx[t]], rstd, brstd, hT, s * TT, bh[s])
        items = []
        groups = []
        for jg in range(FC * 128 // GW):
            grp = {"jg": jg, "wts": None}
            groups.append(grp)
            for jl in range(GW // 128):
                j = jg * (GW // 128) + jl
                for s in range(2):
                    items.append((j, jl, s, grp))
        LOOK = 2

        def load_group(gi):
            if gi >= len(groups) or groups[gi]["wts"] is not None:
                return
            wts = []
            for half in range(2):
                col = half * DFF + groups[gi]["jg"] * GW
                wt, bw, wi = wring.next()
                E.op("pool", lambda e, wt=wt, col=col: e.dma_start(out=wt[:], in_=wup[:, :, col:col + GW]),
                     writes=[bw], dmasem="dwu%d" % wi)
                wts.append((wt, bw, wi, col))
            groups[gi]["wts"] = wts

        state = {}

        def stageA(it):
            j, jl, s, grp = it
            t = 2 * S + s
            gi = groups.index(grp)
            for d_ in range(LOOK + 1):
                load_group(gi + d_)
            wts = grp["wts"]
            res = []
            pw, bpw, _ = C.pW.next()
            for half in range(2):
                wt, bw, _, _ = wts[half]
                ps = pw[:, half * TT:(half + 1) * TT]
                for c in range(DC):
                    E.op("pe", lambda e, c=c, wt=wt, ps=ps: e.matmul(ps, lhsT=wt[:, c, jl * 128:(jl + 1) * 128],
                                                                     rhs=hT[:, c, s * TT:(s + 1) * TT],
                                                                     start=(c == 0), stop=(c == DC - 1)),
                         reads=[bw, bh[s]], writes=[bpw])
                psh = bpsh = None
                if t == 0:
                    psh, bpsh, _ = C.pS.next()
                    for c in range(DC):
                        E.op("pe", lambda e, c=c, wt=wt, psh=psh: e.matmul(psh[:, :HC], lhsT=wt[:, c, jl * 128:(jl + 1) * 128],
                                                                           rhs=hH[:, c, :], start=(c == 0), stop=(c == DC - 1)),
                             reads=[bw, bhh], writes=[bpsh])
                res.append((ps, bpw, psh, bpsh))
            state[(j, s)] = {"pw": (pw, bpw), "ps": res}

        def stageB(it):
            j, jl, s, grp = it
            t = 2 * S + s
            ys = []
            pw, bpw = state[(j, s)]["pw"]
            ue2, bue, _ = uring.next()
            if t == 0:
                for half in range(2):
                    ps, bps, psh, bpsh = state[(j, s)]["ps"][half]
                    E.op("dve", lambda e, psh=psh, half=half: e.tensor_copy(ue2[:, half, 0:HC], psh[:, :HC]),
                         reads=[bpsh], writes=[bue])
            else:
                E.op("pool", lambda e: e.tensor_copy(ue2[:, :, 0:HC], uh[:, j::FC, :]),
                     reads=[buh[j], buh[FC + j]], writes=[bue])
            E.op("act", lambda e: e.activation(ue2[:, :, HC:HC + TT], pw[:, :].rearrange("p (a n) -> p a n", a=2), AF.Copy),
                 reads=[bpw], writes=[bue])
            if t < NT - 1:
                E.op("pool", lambda e: e.tensor_copy(uh[:, j::FC, :], ue2[:, :, TT:TT + HC]),
                     reads=[bue], writes=[buh[j], buh[FC + j]])
            for half in range(2):
                jj = half * FC + j
                ps, bps, psh, bpsh = state[(j, s)]["ps"][half]
                y, by, _ = yring.next()
                E.op("act", lambda e, y=y, ps=ps, jj=jj: e.activation(y[:, :], ps, AF.Identity,
                                                                       bias=cb[:, jj:jj + 1], scale=cw[:, 2, jj:jj + 1]),
                     reads=[bps, bcw, bcb], writes=[by])
                E.op("dve", lambda e, y=y, jj=jj, half=half: e.scalar_tensor_tensor(
                    out=y[:, :], in0=ue2[:, half, 1:1 + TT], scalar=cw[:, 1, jj:jj + 1], in1=y[:, :],
                    op0=ALU.mult, op1=ALU.add), reads=[bue, bcw, by], writes=[by])
                E.op("dve", lambda e, y=y, jj=jj, half=half: e.scalar_tensor_tensor(
                    out=y[:, :], in0=ue2[:, half, 0:TT], scalar=cw[:, 0, jj:jj + 1], in1=y[:, :],
                    op0=ALU.mult, op1=ALU.add), reads=[bue, bcw, by], writes=[by])
                ys.append((y, by))
            state[(j, s)]["ys"] = ys

        def stageC(it):
            j, jl, s, grp = it
            ys = state[(j, s)]["ys"]
            sl, bsl, _ = sring.next()
            E.op("act", lambda e, sl=sl, y=ys[0][0]: e.activation(sl[:, :], y[:, :], AF.Silu),
                 reads=[ys[0][1]], writes=[bsl])
            E.op("pool", lambda e, sl=sl, y=ys[1][0]: e.tensor_tensor(gT[:, j, s * TT:(s + 1) * TT], sl[:, :], y[:, :], ALU.mult),
                 reads=[bsl, ys[1][1]], writes=[bgT[s][j]])
            del state[(j, s)]

        n_it = len(items)
        for k in range(n_it + 2):
            if k < n_it:
                stageA(items[k])
            if 0 <= k - 1 < n_it:
                stageB(items[k - 1])
            if 0 <= k - 2 < n_it:
                stageC(items[k - 2])
        wds = {}

        def load_wd(oc):
            if oc < DC and oc not in wds:
                wt, bw, wi = wdring.next()
                E.op("pool", lambda e, wt=wt, oc=oc: e.dma_start(out=wt[:], in_=wdn[:, :, oc * 128:(oc + 1) * 128]),
                     writes=[bw], dmasem="dwd%d" % wi)
                wds[oc] = (wt, bw)

        load_wd(0)
        for oc in range(DC):
            load_wd(oc + 1)
            wt, bw = wds[oc]
            for s in range(2):
                t = 2 * S + s
                c0 = XH + t * TT
                pw_, bps, _ = C.pW.next()
                ps = pw_[:, 0:TT]
                for k in range(FC):
                    E.op("pe", lambda e, k=k, ps=ps, wt=wt: e.matmul(ps, lhsT=wt[:, k, :],
                                                                     rhs=gT[:, k, s * TT:(s + 1) * TT], start=(k == 0), stop=(k == FC - 1)),
                         reads=[bw, bgT[s][k]], writes=[bps])
                E.op("dve", lambda e, ps=ps, oc=oc, c0=c0: e.tensor_tensor(xres[:, oc, c0:c0 + TT], xres[:, oc, c0:c0 + TT],
                                                                         ps, ALU.add),
                     reads=[bps, bx[t]], writes=[bx[t]])
        for s in range(2):
            t = 2 * S + s
            c0 = XH + t * TT
            if final_g is not None:
                rms_stats(C, lambda c: xres[:, c, c0:c0 + TT], [bx[t]], TT, rstd, brstd)
                for c in range(DC):
                    o, bo, oi = oring.next()
                    E.op("dve", lambda e, c=c, o=o: e.scalar_tensor_tensor(
                        out=o[:, :], in0=xres[:, c, c0:c0 + TT], scalar=fg[:, c:c + 1], in1=rstd[:, :],
                        op0=ALU.mult, op1=ALU.mult), reads=[bx[t], bfg, brstd], writes=[bo])
                    E.op("sp", lambda e, c=c, o=o: e.dma_start(out=outT[c * 128:(c + 1) * 128, t * TT:(t + 1) * TT], in_=o[:, :]),
                         reads=[bo], dmasem="dout%d" % oi)
    if final_g is not None:
        E.finish("sp", oring.b)


def load_xres(C, xT_dram, ncol_halo):
    E = C.E
    xres = C.gsb("xres", [128, DC, ncol_halo + T], F32)
    bxh = E.buf("xh")
    bx = [E.buf("x%d" % t) for t in range(NT)]
    src = xT_dram.rearrange("(c p) t -> p c t", p=128)
    if ncol_halo:
        E.op("sp", lambda e: e.dma_start(out=xres[:, :, 0:ncol_halo], in_=src[:, :, 0:ncol_halo]),
             writes=[bxh], dmasem="dxh")
    for t in range(NT):
        a = ncol_halo + t * TT
        E.op("sp", lambda e, a=a: e.dma_start(out=xres[:, :, a:a + TT], in_=src[:, :, a:a + TT]),
             writes=[bx[t]], dmasem="dx%d" % t)
    return xres, bx, bxh


def store_xres(C, xres, bx, ncol_halo, outT):
    E = C.E
    dst = outT.rearrange("(c p) t -> p c t", p=128)
    for t in range(NT):
        a = ncol_halo + t * TT
        E.op("sp", lambda e, a=a, t=t: e.dma_start(out=dst[:, :, t * TT:(t + 1) * TT], in_=xres[:, :, a:a + TT]),
             reads=[bx[t]], dmasem="dxo%d" % t)
    E.finish("sp", bx)


XH0 = 32
AW = 512
CK = 31


def mixer_phase(C, xres, XH, bx, bxh, W):
    nc, E, sb = C.nc, C.E, C.sb
    st = C.st
    HC = 32
    ld = lambda name, shape, src, key: _ld(C, name, shape, src, key)
    g, bg = ld("mx_g", [128, DC], W["g"], "dc0")
    b_u, bb_u = ld("mx_bu", [128, 4], W["b_u"], "dc1")
    b_ag, bb_ag = ld("mx_bag", [128, 8], W["b_ag"], "dc2")
    bvb, bbvb = ld("mx_bvb", [128, AW], W["b_v_bc"], "dc3")
    vg, bvg = ld("mx_vg", [128, 4], W["vln_g"], "dc0")
    vb, bvb2 = ld("mx_vb", [128, 4], W["vln_b"], "dc1")
    wsT, bwsT = ld("mx_wsT", [128, 8, 128], W["wsT"], "dc2")
    maskT, bmaskT = ld("mx_maskT", [128, 128], W["maskT"], "dc3")
    bsbc, bbsbc = ld("mx_bsbc", [128, 4, 128], W["bs_bc"], "dc0")
    cw, bcw = ld("mx_cw", [128, 4, CK], W["cw"], "dc1")
    cb, bcb = ld("mx_cb", [128, 4], W["cb"], "dc2")
    cg, bcg = ld("mx_cg", [128, 4], W["cln_g"], "dc3")
    cbt, bcbt = ld("mx_cbt", [128, 4], W["cln_b"], "dc0")
    flag, bflag = ld("mx_flag", [128, 1], W["flag"], "dc1")
    for b in (bg, bb_u, bb_ag, bbvb, bvg, bvb2, bwsT, bmaskT, bbsbc, bcw, bcb, bcg, bcbt, bflag):
        b.w = (b.w[0], E.cnt[b.w[0]])
    win = sb("mx_win", [128, DC, 2048], BF16)
    wsrc = W["w_in"].rearrange("(c p) n -> p c n", p=128)
    bwinb = [E.buf() for _ in range(4)]
    for b_ in (2, 3, 0, 1):
        E.op("pool", lambda e, b_=b_: e.dma_start(out=win[:, :, b_ * 512:(b_ + 1) * 512], in_=wsrc[:, :, b_ * 512:(b_ + 1) * 512]),
             writes=[bwinb[b_]], dmasem="dwi%d" % b_)
    wout = sb("mx_wout", [128, DC, D], BF16); bwout = E.buf()
    wosrc = W["w_out"].rearrange("(c p) n -> p c n", p=128)
    for c in range(DC):
        E.op("pool", lambda e, c=c: e.dma_start(out=wout[:, c, :], in_=wosrc[:, c, :]), writes=[bwout],
             dmasem="dwu0", group=True)
    wsm = sb("mx_wsm", [128, 8, 128], BF16); bwsm = E.buf()
    for h in range(8):
        E.op("dve", lambda e, h=h: e.tensor_tensor(wsm[:, h, :], wsT[:, h, :], maskT[:, :], ALU.mult),
             reads=[bwsT, bmaskT], writes=[bwsm])
    Cb = sb("mx_Cb", [128, 4, 128], F32); bCb = E.buf()
    for i in range(4):
        ps, bps, _ = C.pS.next()
        for hh in range(2):
            E.op("pe", lambda e, hh=hh, i=i, ps=ps: e.matmul(ps[hh * 64:(hh + 1) * 64, 0:128], lhsT=C.ones[:, 0:64],
                                                           rhs=wsm[:, 2 * i + hh, :], start=True, stop=True),
                 reads=[C.bones, bwsm], writes=[bps])
        E.op("dve", lambda e, i=i, ps=ps: e.scalar_tensor_tensor(out=Cb[:, i, :], in0=ps[:, 0:128], scalar=vb[:, i:i + 1],
                                                                 in1=bsbc[:, i, :], op0=ALU.mult, op1=ALU.add),
             reads=[bps, bvb2, bbsbc], writes=[bCb])

    hT = sb("mx_h", [128, DC, TT], BF16); bh = E.buf()
    hH = sb("mx_hh", [128, DC, HC], BF16); bhh = E.buf()
    rstd = sb("mx_rstd", [128, TT], F32); brstd = E.buf()
    rstdh = sb("mx_rstdh", [128, HC], F32); brstdh = E.buf()
    uT = sb("mx_uT", [128, 4, TT], F32); buT = [E.buf() for _ in range(4)]
    yaT = sb("mx_yaT", [128, 4, TT], BF16); byaT = E.buf()
    ybT = sb("mx_ybT", [128, 4, TT], BF16); bybT = E.buf()
    hgr = Ring(E, nc, st, "mx_hg", [128, 4, HC + TT], F32, 1)
    ycv = sb("mx_ycv", [128, 4, TT], F32); bycv = [E.buf() for _ in range(4)]
    zring = Ring(E, nc, st, "mx_z", [128, AW], F32, 1)
    vring = Ring(E, nc, st, "mx_v", [128, AW], F32, 2)
    vnring = Ring(E, nc, st, "mx_vn", [128, AW], BF16, 2)
    sgring = Ring(E, nc, st, "mx_sg", [128, TT], F32, 2)
    tmpr = Ring(E, nc, st, "mx_tmp", [128, 128], F32, 3)
    stt = Ring(E, nc, st, "mx_st", [128, 8], F32, 3)
    tmpcv = sb("mx_tmpcv", [128, TT], F32); bptmp = E.buf()
    ybf = Ring(E, nc, st, "mx_ybf", [128, TT], BF16, 3)
    lnm = sb("mx_lnm", [128, TT], F32); blnm = E.buf()
    lnr = sb("mx_lnr", [128, TT], F32); blnr = E.buf()
    lnt = Ring(E, nc, st, "mx_lnt", [128, TT], F32, 2)

    def norm_cols(col0, ncols, xb, r, br, hdst, bhd):
        rms_stats(C, lambda c: xres[:, c, col0:col0 + ncols], xb, ncols, r, br)
        for c in range(DC):
            E.op("dve", lambda e, c=c: e.scalar_tensor_tensor(
                out=hdst[:, c, :ncols], in0=xres[:, c, col0:col0 + ncols], scalar=g[:, c:c + 1],
                in1=r[:, :ncols], op0=ALU.mult, op1=ALU.mult),
                reads=xb + [bg, br], writes=[bhd])

    def glu(hsrc, bhs, ncols, hg, bhg, dcol0, useflag):
        for j in range(4):
            pa, bpa, _ = C.pA.next()
            pg, bpg, _ = C.pB.next()
            for c in range(DC):
                E.op("pe", lambda e, c=c, pa=pa, j=j: e.matmul(pa[:, :ncols], lhsT=win[:, c, 1024 + j * 128:1024 + (j + 1) * 128],
                                                              rhs=hsrc[:, c, :ncols], start=(c == 0), stop=(c == DC - 1)),
                     reads=[bwinb[2], bhs], writes=[bpa])
            for c in range(DC):
                E.op("pe", lambda e, c=c, pg=pg, j=j: e.matmul(pg[:, :ncols], lhsT=win[:, c, 1536 + j * 128:1536 + (j + 1) * 128],
                                                              rhs=hsrc[:, c, :ncols], start=(c == 0), stop=(c == DC - 1)),
                     reads=[bwinb[3], bhs], writes=[bpg])
            sg, bsg, _ = sgring.next()
            E.op("act", lambda e, sg=sg, pg=pg, j=j: e.activation(sg[:, :ncols], pg[:, :ncols], AF.Sigmoid,
                                                                   bias=b_ag[:, 4 + j:5 + j]),
                 reads=[bpg, bb_ag], writes=[bsg])
            E.op("dve", lambda e, sg=sg, pa=pa, j=j: e.scalar_tensor_tensor(
                out=hg[:, j, dcol0:dcol0 + ncols], in0=pa[:, :ncols], scalar=b_ag[:, j:j + 1], in1=sg[:, :ncols],
                op0=ALU.add, op1=ALU.mult), reads=[bpa, bsg, bb_ag], writes=[bhg])
            if useflag:
                E.op("dve", lambda e, j=j: e.tensor_scalar(hg[:, j, dcol0:dcol0 + ncols], hg[:, j, dcol0:dcol0 + ncols],
                                                           flag[:, 0:1], None, ALU.mult),
                     reads=[bflag, bhg], writes=[bhg])

    norm_cols(XH - HC, HC, [bxh], rstdh, brstdh, hH, bhh)
    hg_prev = None
    for t in range(NT):
        c0 = XH + t * TT
        norm_cols(c0, TT, [bx[t]], rstd, brstd, hT, bh)
        E.defer = []
        for j in range(4):
            ps, bps, _ = C.pA.next()
            for c in range(DC):
                E.op("pe", lambda e, c=c, ps=ps, j=j: e.matmul(ps[:, :], lhsT=win[:, c, j * 128:(j + 1) * 128], rhs=hT[:, c, :],
                                                              start=(c == 0), stop=(c == DC - 1)),
                     reads=[bwinb[0], bh], writes=[bps])
            E.op("act", lambda e, ps=ps, j=j: e.activation(uT[:, j, :], ps[:, :], AF.Gelu_apprx_tanh, bias=b_u[:, j:j + 1]),
                 reads=[bps, bb_u], writes=[buT[j]])
        for blk in range(TT // 128):
            ps, bps, _ = C.pB.next()
            for c in range(DC):
                E.op("pe", lambda e, c=c, ps=ps, blk=blk: e.matmul(ps[:, :], lhsT=hT[:, c, blk * 128:(blk + 1) * 128],
                                                                  rhs=win[:, c, 512:1024], start=(c == 0), stop=(c == DC - 1)),
                     reads=[bwinb[1], bh], writes=[bps])
            z, bz, _ = zring.next()
            E.op("dve", lambda e, z=z, ps=ps: e.tensor_tensor(z[:, :], ps[:, :], bvb[:, :], ALU.add),
                 reads=[bps, bbvb], writes=[bz])
            v, bv, _ = vring.next()
            E.op("act", lambda e, z=z, v=v: e.activation(v[:, :], z[:, :], AF.Gelu_apprx_tanh), reads=[bz], writes=[bv])
            s6, bs6, _ = stt.next()
            E.op("dve", lambda e, v=v, s6=s6: e.bn_stats(s6[:, 0:6], v[:, :]), reads=[bv], writes=[bs6])
            m2, bm2, _ = stt.next()
            E.op("dve", lambda e, m2=m2, s6=s6: e.bn_aggr(m2[:, 0:2], s6[:, 0:6]), reads=[bs6], writes=[bm2])
            E.op("act", lambda e, m2=m2: e.activation(m2[:, 2:3], m2[:, 1:2], AF.Sqrt, bias=C.epsb[:, 0:1]),
                 reads=[bm2, C.bepsb], writes=[bm2])
            E.op("dve", lambda e, m2=m2: e.reciprocal(m2[:, 3:4], m2[:, 2:3]), reads=[bm2], writes=[bm2])
            vn, bvn, _ = vnring.next()
            E.op("dve", lambda e, v=v, vn=vn, m2=m2: e.tensor_scalar(vn[:, :], v[:, :], m2[:, 0:1], m2[:, 3:4],
                                                                      ALU.subtract, ALU.mult),
                 reads=[bv, bm2], writes=[bvn])
            for i in range(4):
                pm, bpm, _ = C.pS.next()
                for hh in range(2):
                    h = 2 * i + hh
                    E.op("pe", lambda e, hh=hh, h=h, pm=pm, vn=vn: e.matmul(pm[hh * 64:(hh + 1) * 64, 0:128],
                                                                           lhsT=vn[:, h * 64:(h + 1) * 64], rhs=wsm[:, h, :],
                                                                           start=True, stop=True),
                         reads=[bvn, bwsm], writes=[bpm])
                tm, btm, _ = tmpr.next()
                E.op("dve", lambda e, tm=tm, pm=pm, i=i: e.scalar_tensor_tensor(
                    out=tm[:, :], in0=pm[:, 0:128], scalar=vg[:, i:i + 1], in1=Cb[:, i, :], op0=ALU.mult, op1=ALU.add),
                    reads=[bpm, bvg, bCb], writes=[btm])
                E.op("pool", lambda e, tm=tm, i=i, blk=blk: e.tensor_tensor(yaT[:, i, blk * 128:(blk + 1) * 128], tm[:, :],
                                                                          uT[:, i, blk * 128:(blk + 1) * 128], ALU.mult),
                     reads=[btm, buT[i]], writes=[byaT])
        uv_ops = E.defer
        E.defer = None
        hg, bhg, _ = hgr.next()
        if t == 0:
            glu(hH, bhh, HC, hg, bhg, 0, True)
        else:
            E.op("pool", lambda e, hg=hg, hp=hg_prev[0]: e.tensor_copy(hg[:, :, 0:HC], hp[:, :, TT:TT + HC]),
                 reads=[bhg], writes=[bhg])
        glu(hT, bh, TT, hg, bhg, HC, False)
        hg_prev = (hg, bhg)
        ptmp = tmpcv
        for j in range(4):
            E.op("dve", lambda e, j=j, hg=hg: e.tensor_scalar(ycv[:, j, :], hg[:, j, 2:2 + TT], cw[:, j, 0:1], cb[:, j:j + 1],
                                                            ALU.mult, ALU.add),
                 reads=[bhg, bcw, bcb], writes=[bycv[j]])
        for k in range(1, CK):
            for j in range(3):
                E.op("dve", lambda e, j=j, k=k, hg=hg: e.scalar_tensor_tensor(
                    out=ycv[:, j, :], in0=hg[:, j, 2 + k:2 + k + TT], scalar=cw[:, j, k:k + 1], in1=ycv[:, j, :],
                    op0=ALU.mult, op1=ALU.add), reads=[bhg, bcw, bycv[j]], writes=[bycv[j]])
                E.replay(uv_ops, 2)
            E.op("pool", lambda e, k=k, hg=hg: e.tensor_scalar(ptmp[:, :], hg[:, 3, 2 + k:2 + k + TT], cw[:, 3, k:k + 1], 0.0,
                                                              ALU.mult, ALU.add),
                 reads=[bhg, bcw], writes=[bptmp])
            E.op("pool", lambda e: e.tensor_tensor(ycv[:, 3, :], ycv[:, 3, :], ptmp[:, :], ALU.add),
                 reads=[bptmp, bycv[3]], writes=[bycv[3]])
        E.replay(uv_ops, len(uv_ops))
        p1, bp1, _ = C.pS.next()
        p2, bp2, _ = C.pS.next()
        for j in range(4):
            yb1, byb1, _ = ybf.next()
            E.op("act", lambda e, j=j, yb1=yb1: e.activation(yb1[:, :], ycv[:, j, :], AF.Copy), reads=[bycv[j]], writes=[byb1])
            E.op("pe", lambda e, j=j, yb1=yb1: e.matmul(p1[:, :], lhsT=C.ones[:], rhs=yb1[:, :], start=(j == 0), stop=(j == 3)),
                 reads=[byb1, C.bones], writes=[bp1])
            yb2, byb2, _ = ybf.next()
            E.op("act", lambda e, j=j, yb2=yb2: e.activation(yb2[:, :], ycv[:, j, :], AF.Square), reads=[bycv[j]], writes=[byb2])
            E.op("pe", lambda e, j=j, yb2=yb2: e.matmul(p2[:, :], lhsT=C.ones[:], rhs=yb2[:, :], start=(j == 0), stop=(j == 3)),
                 reads=[byb2, C.bones], writes=[bp2])
        E.op("act", lambda e: e.activation(lnm[:, :], p1[:, :], AF.Copy, scale=1.0 / AW), reads=[bp1], writes=[blnm])
        lt, blt, _ = lnt.next()
        E.op("act", lambda e, lt=lt: e.activation(lt[:, :], p1[:, :], AF.Square, scale=1.0 / AW), reads=[bp1], writes=[blt])
        E.op("dve", lambda e, lt=lt: e.scalar_tensor_tensor(out=lnr[:, :], in0=p2[:, :], scalar=1.0 / AW, in1=lt[:, :],
                                                            op0=ALU.mult, op1=ALU.subtract),
             reads=[bp2, blt], writes=[blnr])
        E.op("act", lambda e: e.activation(lnr[:, :], lnr[:, :], AF.Sqrt, bias=C.epsb[:, 0:1]), reads=[blnr, C.bepsb], writes=[blnr])
        E.op("dve", lambda e: e.reciprocal(lnr[:, :], lnr[:, :]), reads=[blnr], writes=[blnr])
        for j in range(4):
            lt, blt, _ = lnt.next()
            E.op("dve", lambda e, j=j, lt=lt: e.tensor_tensor(lt[:, :], ycv[:, j, :], lnm[:, :], ALU.subtract),
                 reads=[bycv[j], blnm], writes=[blt])
            E.op("pool", lambda e, lt=lt: e.tensor_tensor(lt[:, :], lt[:, :], lnr[:, :], ALU.mult),
                 reads=[blt, blnr], writes=[blt])
            E.op("act", lambda e, j=j, lt=lt: e.activation(ybT[:, j, :], lt[:, :], AF.Silu, bias=cbt[:, j:j + 1], scale=cg[:, j:j + 1]),
                 reads=[blt, bcg, bcbt], writes=[bybT])
        for oc in range(DC):
            ps, bps, _ = C.pA.next()
            for k in range(DC):
                src, bsrc = (yaT, byaT) if k < 4 else (ybT, bybT)
                E.op("pe", lambda e, k=k, ps=ps, oc=oc, src=src: e.matmul(ps[:, :], lhsT=wout[:, k, oc * 128:(oc + 1) * 128],
                                                                         rhs=src[:, k % 4, :], start=(k == 0), stop=(k == DC - 1)),
                     reads=[bwout, bsrc], writes=[bps])
            E.op("dve", lambda e, ps=ps, oc=oc: e.tensor_tensor(xres[:, oc, c0:c0 + TT], xres[:, oc, c0:c0 + TT], ps[:, :], ALU.add),
                 reads=[bps, bx[t]], writes=[bx[t]])


def _ld(C, name, shape, src, key):
    t = C.sb(name, shape, F32)
    b = C.E.buf(name)
    C.E.op("sp", lambda e: e.dma_start(out=t[:], in_=src), writes=[b], dmasem=key)
    return t, b


def build_mixer():
    nc = bass.Bass("TRN2", target_bir_lowering=False)
    dt = lambda n, s, k="ExternalInput": nc.dram_tensor(n, s, F32, kind=k).ap()
    xT = dt("xT", [D, XH0 + T])
    W = {"g": dt("g", [128, DC]), "b_u": dt("b_u", [128, 4]), "b_ag": dt("b_ag", [128, 8]), "b_v_bc": dt("b_v_bc", [128, AW]),
         "vln_g": dt("vln_g", [128, 4]), "vln_b": dt("vln_b", [128, 4]), "wsT": dt("wsT", [128, 8, 128]),
         "maskT": dt("maskT", [128, 128]), "bs_bc": dt("bs_bc", [128, 4, 128]), "cw": dt("cw", [128, 4, CK]),
         "cb": dt("cb", [128, 4]), "cln_g": dt("cln_g", [128, 4]), "cln_b": dt("cln_b", [128, 4]), "flag": dt("flag", [128, 1]),
         "w_in": dt("w_in", [D, 2048]), "w_out": dt("w_out", [D, D])}
    outT = dt("outT", [D, T], "ExternalOutput")
    with ExitStack() as st:
        C = make_ctx(nc, st)
        xres, bx, bxh = load_xres(C, xT, XH0)
        phase_begin(C)
        mixer_phase(C, xres, XH0, bx, bxh, W)
        phase_end(C)
        store_xres(C, xres, bx, XH0, outT)
    return nc


def mixer_consts(inp):
    b_in = inp["even_b_in"][0]
    m = {"g": pvec(inp["even_norm_g"][0]), "b_u": pvec(b_in[0:512]), "b_ag": pvec(b_in[1024:2048]),
         "b_v_bc": np.ascontiguousarray(np.broadcast_to(b_in[512:1024][None, :], (128, AW))),
         "vln_g": pvec(inp["even_v_ln_g"][0]), "vln_b": pvec(inp["even_v_ln_b"][0]),
         "wsT": np.ascontiguousarray(inp["even_w_s"][0].transpose(2, 0, 1)),
         "maskT": np.triu(np.ones((128, 128), np.float32)),
         "bs_bc": np.ascontiguousarray(np.repeat(inp["even_b_s"][0], 64, axis=0).reshape(4, 128, 128).transpose(1, 0, 2)),
         "cw": np.ascontiguousarray(inp["even_conv_w"][0].reshape(CK, 4, 128).transpose(2, 1, 0)),
         "cb": pvec(inp["even_conv_b"][0]), "cln_g": pvec(inp["even_conv_ln_g"][0]), "cln_b": pvec(inp["even_conv_ln_b"][0]),
         "w_in": np.ascontiguousarray(inp["even_w_in"][0]), "w_out": np.ascontiguousarray(inp["even_w_out"][0])}
    return m


def mixer_host_inputs(x, inp):
    base = mixer_consts(inp)
    maps = []
    for core in range(NCORES):
        b, h = divmod(core, 2)
        xs = x[b, h * T:(h + 1) * T]
        halo = x[b, T - XH0:T] if h == 1 else np.zeros((XH0, D), np.float32)
        m = dict(base)
        m["xT"] = np.ascontiguousarray(np.concatenate([halo, xs], axis=0).T)
        m["flag"] = np.full((128, 1), float(h), np.float32)
        maps.append(m)
    return maps


def run_mixer(x, inp):
    if "mixer" not in _NC_CACHE:
        _NC_CACHE["mixer"] = build_mixer()
    res = run_bass_kernel_spmd(_NC_CACHE["mixer"], mixer_host_inputs(x, inp), core_ids=list(range(NCORES)))
    return gather_T(res)


import os
ATT_STAGE = int(os.environ.get('ATT_STAGE', '9'))
ATT_SUB = int(os.environ.get('ATT_SUB', '9'))
NH2 = 8
TX = 2 * T


def qkv_phase(C, xres, XH, bx, W, qT, bqT, ksnd, vsnd, bks, bvs):
    nc, E, sb, st = C.nc, C.E, C.sb, C.st
    g, bg = _ld(C, "qk_g", [128, DC], W["g"], "dc0")
    ctab, bct = _ld(C, "qk_ct", [128, T], W["ctab"], "dc1")
    stab, bst = _ld(C, "qk_st", [128, T], W["stab"], "dc2")
    perm, bpm = _ld(C, "qk_pm", [128, 128], W["perm"], "dc3")
    wq = sb("qk_w", [128, DC, 3 * D], BF16)
    wsrc = W["wqkv"].rearrange("(c p) n -> p c n", p=128)
    bwqb = [E.buf() for _ in range(6)]
    for b_ in range(6):
        E.op("pool", lambda e, b_=b_: e.dma_start(out=wq[:, :, b_ * 512:(b_ + 1) * 512], in_=wsrc[:, :, b_ * 512:(b_ + 1) * 512]),
             writes=[bwqb[b_]], dmasem="dwq%d" % b_)
    hT = sb("qk_h", [128, DC, TT], BF16); bh = E.buf()
    rstd = sb("qk_rstd", [128, TT], F32); brstd = E.buf()
    qfr = Ring(E, nc, st, "qk_qf", [128, TT], F32, 3)
    t1r = Ring(E, nc, st, "qk_t1", [128, TT], F32, 2)
    t2r = Ring(E, nc, st, "qk_t2", [128, TT], F32, 2)
    stg = Ring(E, nc, st, "qk_stg", [128, TT], BF16, 4)
    for t in range(NT):
        c0 = XH + t * TT
        tc0 = t * TT
        rms_stats(C, lambda c: xres[:, c, c0:c0 + TT], [bx[t]], TT, rstd, brstd)
        for c in range(DC):
            E.op("dve", lambda e, c=c: e.scalar_tensor_tensor(out=hT[:, c, :], in0=xres[:, c, c0:c0 + TT], scalar=g[:, c:c + 1],
                                                              in1=rstd[:, :], op0=ALU.mult, op1=ALU.mult),
                 reads=[bx[t], bg, brstd], writes=[bh])
        def partA(j):
            ps, bps, _ = C.pA.next()
            for c in range(DC):
                E.op("pe", lambda e, c=c, ps=ps, j=j: e.matmul(ps[:, :], lhsT=wq[:, c, j * 128:(j + 1) * 128], rhs=hT[:, c, :],
                                                              start=(c == 0), stop=(c == DC - 1)),
                     reads=[bwqb[j // 4], bh], writes=[bps])
            if j < 16:
                qf, bqf, _ = qfr.next()
                E.op("act", lambda e, qf=qf, ps=ps: e.activation(qf[:, :], ps[:, :], AF.Copy), reads=[bps], writes=[bqf])
                return (qf, bqf)
            sg, bsg, si = stg.next()
            E.op("act", lambda e, sg=sg, ps=ps: e.activation(sg[:, :], ps[:, :], AF.Copy), reads=[bps], writes=[bsg])
            E.op("sp", lambda e, sg=sg, j=j: e.dma_start(out=vsnd(j - 16)[:, tc0:tc0 + TT], in_=sg[:, :]),
                 reads=[bsg], writes=[bvs], dmasem="dst%d" % si, group=True)
            return None

        def partB(j, st_):
            qf, bqf = st_
            pq, bpq, _ = C.pB.next()
            E.op("pe", lambda e, pq=pq, qf=qf: e.matmul(pq[:, :], lhsT=perm[:, :], rhs=qf[:, :], start=True, stop=True),
                 reads=[bpm, bqf], writes=[bpq])
            t1, bt1, _ = t1r.next()
            E.op("pool", lambda e, t1=t1, qf=qf: e.tensor_tensor(t1[:, :], qf[:, :], ctab[:, tc0:tc0 + TT], ALU.mult),
                 reads=[bqf, bct], writes=[bt1])
            t2, bt2, _ = t2r.next()
            E.op("dve", lambda e, t2=t2, pq=pq: e.tensor_tensor(t2[:, :], pq[:, :], stab[:, tc0:tc0 + TT], ALU.mult),
                 reads=[bpq, bst], writes=[bt2])
            if j < 8:
                E.op("dve", lambda e, t1=t1, t2=t2, j=j: e.tensor_tensor(qT[:, j, tc0:tc0 + TT], t1[:, :], t2[:, :], ALU.add),
                     reads=[bt1, bt2], writes=[bqT])
            else:
                sg, bsg, si = stg.next()
                E.op("dve", lambda e, t1=t1, t2=t2, sg=sg: e.tensor_tensor(sg[:, :], t1[:, :], t2[:, :], ALU.add),
                     reads=[bt1, bt2], writes=[bsg])
                E.op("sp", lambda e, sg=sg, j=j: e.dma_start(out=ksnd(j - 8)[:, tc0:tc0 + TT], in_=sg[:, :]),
                     reads=[bsg], writes=[bks], dmasem="dst%d" % si, group=True)

        pend = None
        for j in range(24):
            st_ = partA(j)
            if pend is not None:
                partB(*pend)
            pend = (j, st_) if st_ is not None else None
        if pend is not None:
            partB(*pend)
    evs = {}
    for i in range(4):
        k = "dst%d" % i
        if k in E.cnt:
            evs[k] = E.cnt[k]
    C.kv_store_events = evs


def wait_events(C, e, evs):
    C.E._wait(e, dict(evs))


def attn_phase(C, qT, bqT, kprev, kown, vprev, vown, kv_events, oT, boT, W, vaug_dram=None):
    nc, E, sb, st = C.nc, C.E, C.sb, C.st
    ps_t = lambda n, s, d: st.enter_context(nc.psum_tensor(C.pf + n, s, d))
    acc = [ps_t("acc%d" % i, [128, 512], F32) for i in range(4)]
    bacc = [E.buf("acc%d" % i) for i in range(4)]
    sring = Ring(E, nc, st, "at_S", [128, 512], F32, 2, psum=True)
    ptring = Ring(E, nc, st, "at_pt", [128, 4, 128], F32, 2, psum=True)
    mk, bmk = _ld(C, "at_mk", [128, 256], W["mask"], "dc0")
    flag, bflag = _ld(C, "at_flag", [128, 1], W["flag"], "dc1")
    idf, bidf = _ld(C, "at_id", [128, 128], W["ident"], "dc2")
    ident = sb("at_idb", [128, 128], BF16); bident = E.buf()
    E.op("dve", lambda e: e.tensor_copy(ident[:, :], idf[:, :]), reads=[bidf], writes=[bident])
    mnorm = sb("at_mn", [128, 256], BF16); bmn = E.buf()
    mfirst = sb("at_mf", [128, 256], BF16); bmf = E.buf()
    E.op("dve", lambda e: e.tensor_copy(mnorm[:, :], mk[:, :]), reads=[bmk], writes=[bmn])
    E.op("dve", lambda e: e.tensor_copy(mfirst[:, 128:256], mk[:, 128:256]), reads=[bmk], writes=[bmf])
    E.op("dve", lambda e: e.tensor_scalar(mfirst[:, 0:128], mk[:, 0:128], flag[:, 0:1], None, ALU.mult),
         reads=[bmk, bflag], writes=[bmf])
    kx = Ring(E, nc, st, "at_kx", [128, TX], BF16, 2)
    vx = Ring(E, nc, st, "at_vx", [128, TX], BF16, 2)
    NBLK = 69
    vaug = sb("at_vaug", [128, NBLK, 3, 64], BF16); bvaug = E.buf()
    NVB = (NBLK + 3) // 4
    bvb = [E.buf() for _ in range(NVB)]
    if vaug_dram is None:
        E.op("pool", lambda e: e.memset(vaug[:, :, 1, :], 1.0), writes=[bvaug] + bvb)
    pr = Ring(E, nc, st, "at_P", [128, 256], BF16, 4)
    rl = sb("at_rl", [128, T], F32); brl = [E.buf() for _ in range(4)]

    slots = {}

    def cols(kind, a, r):
        if kind == 0:
            return slice(128 * a, 128 * a + 128)
        if kind == 1:
            return slice(512 * a + r, 512 * a + r + 4 * 127 + 1, 4)
        return slice(2048 * a + r, 2048 * a + r + 16 * 127 + 1, 16)

    blocks = []
    for a in range(15, 32):
        blocks.append((0, a, 0))
    for a in range(3, 8):
        for r in range(4):
            blocks.append((1, a, r))
    for a in range(2):
        for r in range(16):
            blocks.append((2, a, r))
    assert len(blocks) == NBLK
    for i, b in enumerate(blocks):
        slots[b] = i

    for c in range((NH2 if ATT_STAGE >= 9 else 1) if ATT_SUB >= 1 else 0):
        kt, bkt, ki = kx.next()
        vt, bvt, vi = vx.next()
        wait_events(C, "sp", kv_events(c) if callable(kv_events) else kv_events)
        E.op("sp", lambda e, kt=kt: e.dma_start(out=kt[:, 0:T], in_=kprev(c)), writes=[bkt], dmasem="dkx%d" % ki, group=True)
        E.op("sp", lambda e, kt=kt: e.dma_start(out=kt[:, T:TX], in_=kown(c)), writes=[bkt], dmasem="dkx%d" % ki, group=True)
        if vaug_dram is None:
            E.op("sp", lambda e, vt=vt: e.dma_start(out=vt[:, 0:T], in_=vprev(c)), writes=[bvt], dmasem="dvx%d" % vi, group=True)
            E.op("sp", lambda e, vt=vt: e.dma_start(out=vt[:, T:TX], in_=vown(c)), writes=[bvt], dmasem="dvx%d" % vi, group=True)
        else:
            E.op("sp", lambda e: e.dma_start(out=vaug[:, :, :, :].rearrange("p b a d -> p (b a d)"), in_=vaug_dram[c]), writes=[bvaug], dmasem="dvx0")
        nb_ = len(blocks) if vaug_dram is None else 0
        for bi, i0 in enumerate(range(0, nb_, 4)):
            n = min(4, nb_ - i0)
            pt, bpt, _ = ptring.next()
            for q_ in range(n):
                kind, a, r = blocks[i0 + q_]
                E.op("pe", lambda e, q_=q_, kind=kind, a=a, r=r: e.matmul(pt[:, q_, :], lhsT=vt[:, cols(kind, a, r)], rhs=ident[:, :],
                                                                       start=True, stop=True),
                     reads=[bvt, bident], writes=[bpt])
            src = pt[:, 0:n, :].rearrange("p n (a d) -> p n a d", a=2)
            if bi % 2 == 0:
                E.op("act", lambda e: e.activation(vaug[:, i0:i0 + n, 0:3:2, :], src, AF.Copy), reads=[bpt], writes=[bvb[bi]])
            else:
                E.op("dve", lambda e: e.tensor_copy(vaug[:, i0:i0 + n, 0:3:2, :], src), reads=[bpt], writes=[bvb[bi]])
        for hh in range(2 if ATT_STAGE >= 2 else 0):
            pp = slice(64 * hh, 64 * hh + 64)
            started = [False] * 4

            def lhs_v(slot):
                return vaug[:, slot, hh:hh + 2, :].rearrange("p a d -> p (a d)")

            units = []
            for qb in range(16):
                units.append((slice(128 * qb, 128 * qb + 128), (0, 15 + qb, 0), (0, 16 + qb, 0), qb == 0,
                              [(qb // 4, slice((qb % 4) * 128, (qb % 4) * 128 + 128), 0, 128)]))
            for a in range(4):
                for r in range(4):
                    units.append((slice(512 * a + r, 512 * a + r + 509, 4), (1, 3 + a, r), (1, 4 + a, r), a == 0,
                                  [(a, slice(r, r + 509, 4), 0, 128)]))
            for r in range(16):
                units.append((slice(r, r + 16 * 127 + 1, 16), (2, 0, r), (2, 1, r), True,
                              [(gq, slice(r, r + 16 * 31 + 1, 16), 32 * gq, 32) for gq in range(4)]))
            ust = {}

            def stA(i):
                qsl, kprev_b, kcur_b, first, outs = units[i]
                S, bS, _ = sring.next()
                E.op("pe", lambda e: e.matmul(S[:, 0:128], lhsT=kt[pp, cols(*kprev_b)], rhs=qT[pp, c, qsl], start=True, stop=True),
                     reads=[bkt, bqT], writes=[bS])
                E.op("pe", lambda e: e.matmul(S[:, 128:256], lhsT=kt[pp, cols(*kcur_b)], rhs=qT[pp, c, qsl], start=True, stop=True),
                     reads=[bkt, bqT], writes=[bS])
                ust[i] = (S, bS)

            def stB(i):
                qsl, kprev_b, kcur_b, first, outs = units[i]
                S, bS = ust[i]
                P, bP, _ = pr.next()
                E.op("act", lambda e: e.activation(P[:, :], S[:, 0:256], AF.Exp, scale=0.125), reads=[bS], writes=[bP])
                m, bm = (mfirst, bmf) if first else (mnorm, bmn)
                E.op("dve", lambda e: e.tensor_tensor(P[:, :], P[:, :], m[:, :], ALU.mult), reads=[bP, bm], writes=[bP])
                ust[i] = (P, bP)

            def stC(i):
                qsl, kprev_b, kcur_b, first, outs = units[i]
                P, bP = ust.pop(i)
                for (bank, osl, po, n) in outs:
                    for part, kb in ((0, kprev_b), (1, kcur_b)):
                        stt = not started[bank]
                        started[bank] = True
                        E.op("pe", lambda e, bank=bank, osl=osl, po=po, n=n, part=part, kb=kb, stt=stt: e.matmul(
                            acc[bank][:, osl], lhsT=lhs_v(slots[kb]), rhs=P[:, part * 128 + po:part * 128 + po + n],
                            start=stt, stop=False, skip_group_check=True),
                            reads=[bP, bvaug, bvb[slots[kb] // 4]], writes=[bacc[bank]])

            nu = len(units) if ATT_STAGE >= 4 else 0
            for k in range(nu + 2):
                if k < nu:
                    stA(k)
                if 0 <= k - 1 < nu:
                    stB(k - 1)
                if 0 <= k - 2 < nu:
                    stC(k - 2)
            lp = slice(64 * (1 - hh), 64 * (1 - hh) + 64)
            if ATT_STAGE < 5:
                continue
            for gq in range(4):
                cs = slice(gq * 512, (gq + 1) * 512)
                E.op("act", lambda e, gq=gq, cs=cs: e.activation(rl[pp, cs], acc[gq][lp, :], AF.Ln),
                     reads=[bacc[gq]], writes=[brl[gq]])
                E.op("act", lambda e, cs=cs: e.activation(rl[pp, cs], rl[pp, cs], AF.Exp, scale=-1.0),
                     reads=[brl[gq]], writes=[brl[gq]])
                E.op("dve", lambda e, gq=gq, cs=cs: e.tensor_tensor(oT[pp, c, cs], acc[gq][pp, :], rl[pp, cs], ALU.mult),
                     reads=[bacc[gq], brl[gq]], writes=[boT])


def oproj_phase(C, xres, XH, bx, oT, boT, wo_dram):
    nc, E, sb = C.nc, C.E, C.sb
    wo = sb("op_w", [128, DC, D], BF16)
    wsrc = wo_dram.rearrange("(c p) n -> p c n", p=128)
    bwob = [E.buf() for _ in range(4)]
    for b_ in range(4):
        E.op("pool", lambda e, b_=b_: e.dma_start(out=wo[:, :, b_ * 256:(b_ + 1) * 256], in_=wsrc[:, :, b_ * 256:(b_ + 1) * 256]),
             writes=[bwob[b_]], dmasem="dwo%d" % b_)
    for t in range(NT):
        c0 = XH + t * TT
        for oc in range(DC):
            ps, bps, _ = C.pA.next()
            for k in range(DC):
                E.op("pe", lambda e, k=k, ps=ps, oc=oc: e.matmul(ps[:, :], lhsT=wo[:, k, oc * 128:(oc + 1) * 128],
                                                                 rhs=oT[:, k, t * TT:(t + 1) * TT], start=(k == 0), stop=(k == DC - 1)),
                     reads=[bwob[oc // 2], boT], writes=[bps])
            E.op("dve", lambda e, ps=ps, oc=oc: e.tensor_tensor(xres[:, oc, c0:c0 + TT], xres[:, oc, c0:c0 + TT], ps[:, :], ALU.add),
                 reads=[bps, bx[t]], writes=[bx[t]])


def v_blocks_host(v_ext):
    blocks = [(0, a, 0) for a in range(15, 32)] + [(1, a, r) for a in range(3, 8) for r in range(4)] + \
             [(2, a, r) for a in range(2) for r in range(16)]
    idx = []
    for kind, a, r in blocks:
        if kind == 0:
            idx.append(np.arange(128 * a, 128 * a + 128))
        elif kind == 1:
            idx.append(512 * a + r + 4 * np.arange(128))
        else:
            idx.append(2048 * a + r + 16 * np.arange(128))
    idx = np.stack(idx)
    vb = v_ext[idx]
    vb = vb.reshape(69, 128, 8, 2, 64)
    out = np.ones((8, 128, 69, 3, 64), dtype=v_ext.dtype)
    out[:, :, :, 0, :] = vb[:, :, :, 0, :].transpose(2, 1, 0, 3)
    out[:, :, :, 2, :] = vb[:, :, :, 1, :].transpose(2, 1, 0, 3)
    return np.ascontiguousarray(out.reshape(8, 128, 69 * 192))


def attn_consts(half):
    j = np.arange(32, dtype=np.float32)
    inv = (np.float32(10000.0) ** (-j / np.float32(32))).astype(np.float32)
    pos = (half * T + np.arange(T)).astype(np.float32)
    ang = pos[None, :] * inv[:, None]
    cos = np.cos(ang).astype(np.float32)
    sin = np.sin(ang).astype(np.float32)
    ctab = np.concatenate([cos, cos, cos, cos], axis=0)
    stab = np.concatenate([-sin, sin, -sin, sin], axis=0)
    perm = np.zeros((128, 128), np.float32)
    for m in range(128):
        d = m % 64
        k = m - d + ((d + 32) % 64)
        perm[k, m] = 1.0
    kk = np.arange(128)[:, None]
    qq = np.arange(128)[None, :]
    mask = np.concatenate([(kk >= qq), (kk <= qq)], axis=1).astype(np.float32)
    return {"ctab": np.ascontiguousarray(ctab), "stab": np.ascontiguousarray(stab), "perm": perm, "mask": mask,
            "ident": np.eye(128, dtype=np.float32), "flag": np.full((128, 1), float(half), np.float32)}


def dram_in(nc, n, s, dt=F32):
    return nc.dram_tensor(n, s, dt, kind="ExternalInput").ap()


def dram_out(nc, n, s, dt=F32):
    return nc.dram_tensor(n, s, dt, kind="ExternalOutput").ap()


def rowfn(ap):
    return lambda c: ap[c * 128:(c + 1) * 128, :]


def attn_weight_aps(nc):
    return {"g": dram_in(nc, "a_g", [128, DC]), "ctab": dram_in(nc, "a_ctab", [128, T]), "stab": dram_in(nc, "a_stab", [128, T]),
            "perm": dram_in(nc, "a_perm", [128, 128]), "wqkv": dram_in(nc, "a_wqkv", [D, 3 * D]),
            "mask": dram_in(nc, "a_mask", [128, 256]),
            "ident": dram_in(nc, "a_ident", [128, 128]), "wo": dram_in(nc, "a_wo", [D, D])}


def attn_host_consts(inp, half):
    c = attn_consts(half)
    return {"a_g": pvec(inp["odd_norm_g"][0]), "a_ctab": c["ctab"], "a_stab": c["stab"], "a_perm": c["perm"],
            "a_wqkv": np.ascontiguousarray(inp["odd_w_qkv"][0]), "a_mask": c["mask"], "a_flag": c["flag"],
            "a_ident": c["ident"], "a_wo": np.ascontiguousarray(inp["odd_w_o"][0])}


def build_attn_test():
    nc = bass.Bass("TRN2", target_bir_lowering=False)
    xT = dram_in(nc, "xT", [D, 2 + T])
    W = attn_weight_aps(nc)
    W["flag"] = dram_in(nc, "a_flag", [128, 1])
    kprev = dram_in(nc, "kprev", [D, T], BF16)
    vprev = dram_in(nc, "vprev", [D, T], BF16)
    ksnd = nc.dram_tensor("ksnd", [D, T], BF16).ap()
    vsnd = nc.dram_tensor("vsnd", [D, T], BF16).ap()
    vaug_d = dram_in(nc, "vaug", [NH2, 128, 69 * 192], BF16)
    outT = dram_out(nc, "outT", [D, T])
    with ExitStack() as st:
        C = make_ctx(nc, st)
        E = C.E
        xres, bx, bxh = load_xres(C, xT, 2)
        qT = C.gsb("qT", [128, NH2, T], BF16); bqT = E.buf("qT")
        bks = E.buf("ks"); bvs = E.buf("vs")
        phase_begin(C)
        qkv_phase(C, xres, 2, bx, W, qT, bqT, rowfn(ksnd), rowfn(vsnd), bks, bvs)
        phase_end(C)
        oT = C.gsb("oT", [128, NH2, T], BF16); boT = E.buf("oT")
        if ATT_STAGE >= 1:
            phase_begin(C, psum=False)
            attn_phase(C, qT, bqT, rowfn(kprev), rowfn(ksnd), rowfn(vprev), rowfn(vsnd), {}, oT, boT, W, vaug_dram=(vaug_d if os.environ.get('HOSTV') else None))
            phase_end(C)
        if ATT_STAGE >= 9:
            phase_begin(C)
            oproj_phase(C, xres, 2, bx, oT, boT, W["wo"])
            phase_end(C)
        store_xres(C, xres, bx, 2, outT)
    return nc


def build_ffn(final):
    nc = bass.Bass("TRN2", target_bir_lowering=False)
    dt = lambda n, s, k="ExternalInput": nc.dram_tensor(n, s, F32, kind=k).ap()
    xT = dt("xT", [D, 2 + T])
    W = {"g": dt("g", [128, DC]), "wup": dt("wup", [D, 2 * DFF]), "cw": dt("cw", [128, 3, 2 * FC]),
         "cb": dt("cb", [128, 2 * FC]), "wdn": dt("wdn", [DFF, D])}
    fg = dt("fg", [128, DC]) if final else None
    outT = dt("outT", [D, T], "ExternalOutput")
    with ExitStack() as st:
        C = make_ctx(nc, st)
        xres, bx, bxh = load_xres(C, xT, 2)
        phase_begin(C, psum="ffn")
        if final:
            ffn_phase(C, xres, 2, bx, bxh, W, final_g=fg, outT=outT)
            phase_end(C)
        else:
            ffn_phase(C, xres, 2, bx, bxh, W)
            phase_end(C)
            store_xres(C, xres, bx, 2, outT)
    return nc


def pvec(v):
    return np.ascontiguousarray(v.reshape(-1, 128).T)


def ffn_host_inputs(xmid, i, inp, final):
    maps = []
    cw = inp["ffn_conv_w"][i]
    cwl = np.ascontiguousarray(cw.reshape(3, 2 * FC, 128).transpose(2, 0, 1))
    cbl = pvec(inp["ffn_conv_b"][i])
    for core in range(NCORES):
        b, h = divmod(core, 2)
        xs = xmid[b, h * T:(h + 1) * T]
        halo = xmid[b, T - 2:T] if h == 1 else np.zeros((2, D), np.float32)
        xT = np.ascontiguousarray(np.concatenate([halo, xs], axis=0).T)
        m = {"xT": xT, "g": pvec(inp["ffn_norm_g"][i]), "wup": np.ascontiguousarray(inp["ffn_w_up"][i]),
             "cw": cwl, "cb": cbl, "wdn": np.ascontiguousarray(inp["ffn_w_down"][i])}
        if final:
            m["fg"] = pvec(inp["final_norm_g"])
        maps.append(m)
    return maps


def gather_T(res, key="outT"):
    out = np.empty((4, 2 * T, D), np.float32)
    for core in range(NCORES):
        b, h = divmod(core, 2)
        out[b, h * T:(h + 1) * T] = res.results[core][key].T
    return out


def run_ffn(xmid, i, inp, final):
    key = ("ffn", final)
    if key not in _NC_CACHE:
        _NC_CACHE[key] = build_ffn(final)
    res = run_bass_kernel_spmd(_NC_CACHE[key], ffn_host_inputs(xmid, i, inp, final), core_ids=list(range(NCORES)))
    return gather_T(res)

def build_qkv():
    nc = bass.Bass("TRN2", target_bir_lowering=False)
    xT = dram_in(nc, "xT", [D, 2 + T])
    W = attn_weight_aps(nc)
    qo = dram_out(nc, "qo", [D, T], BF16)
    ko = dram_out(nc, "ko", [D, T], BF16)
    vo = dram_out(nc, "vo", [D, T], BF16)
    with ExitStack() as st:
        C = make_ctx(nc, st)
        E = C.E
        xres, bx, bxh = load_xres(C, xT, 2)
        qT = C.gsb("qT", [128, NH2, T], BF16); bqT = E.buf("qT")
        bks = E.buf("ks"); bvs = E.buf("vs")
        phase_begin(C)
        qkv_phase(C, xres, 2, bx, W, qT, bqT, rowfn(ko), rowfn(vo), bks, bvs)
        phase_end(C)
        bq2 = E.buf()
        E.op("sp", lambda e: e.dma_start(out=qo.rearrange("(c p) t -> p c t", p=128), in_=qT[:, :, :]), reads=[bqT], writes=[bq2], dmasem="dxo0")
        E.finish("sp", [bq2, bks, bvs])
        barrier(C)
    return nc


def build_attn():
    nc = bass.Bass("TRN2", target_bir_lowering=False)
    xT = dram_in(nc, "xT", [D, 2 + T])
    W = attn_weight_aps(nc)
    qi = dram_in(nc, "qi", [D, T], BF16)
    kprev = dram_in(nc, "kprev", [D, T], BF16)
    kown = dram_in(nc, "kown", [D, T], BF16)
    vaug_d = dram_in(nc, "vaug", [NH2, 128, 69 * 192], BF16)
    outT = dram_out(nc, "outT", [D, T])
    with ExitStack() as st:
        C = make_ctx(nc, st)
        E = C.E
        xres, bx, bxh = load_xres(C, xT, 2)
        qT = C.gsb("qT", [128, NH2, T], BF16); bqT = E.buf("qT")
        E.op("sp", lambda e: e.dma_start(out=qT[:, :, :], in_=qi.rearrange("(c p) t -> p c t", p=128)), writes=[bqT], dmasem="dxh")
        oT = C.gsb("oT", [128, NH2, T], BF16); boT = E.buf("oT")
        phase_begin(C, psum=False)
        attn_phase(C, qT, bqT, rowfn(kprev), rowfn(kown), None, None, {}, oT, boT, W, vaug_dram=vaug_d)
        phase_end(C)
        phase_begin(C)
        oproj_phase(C, xres, 2, bx, oT, boT, W["wo"])
        phase_end(C)
        store_xres(C, xres, bx, 2, outT)
    return nc


def xT_maps(xfull, halo):
    out = []
    for core in range(NCORES):
        b, h = divmod(core, 2)
        xs = xfull[b, h * T:(h + 1) * T]
        hl = xfull[b, T - halo:T] if h == 1 else np.zeros((halo, D), np.float32)
        out.append(np.ascontiguousarray(np.concatenate([hl, xs], axis=0).T))
    return out


def run_qkv(x1, inp):
    if "qkv" not in _NC_CACHE:
        _NC_CACHE["qkv"] = build_qkv()
    xs = xT_maps(x1, 2)
    maps = []
    for core in range(NCORES):
        m = attn_host_consts(inp, core % 2)
        m["xT"] = xs[core]
        maps.append(m)
    res = run_bass_kernel_spmd(_NC_CACHE["qkv"], maps, core_ids=list(range(NCORES)))
    return [(r["qo"], r["ko"], r["vo"]) for r in res.results]


def run_attn(x1, qkv, inp):
    if "attn" not in _NC_CACHE:
        _NC_CACHE["attn"] = build_attn()
    xs = xT_maps(x1, 2)
    maps = []
    for core in range(NCORES):
        b, h = divmod(core, 2)
        m = attn_host_consts(inp, h)
        m["xT"] = xs[core]
        q, k, v = qkv[core]
        kp, vp = qkv[2 * b][1], qkv[2 * b][2]
        m["qi"] = q
        m["kown"] = k
        m["kprev"] = kp
        vext = np.concatenate([np.ascontiguousarray(vp.T), np.ascontiguousarray(v.T)], axis=0)
        m["vaug"] = v_blocks_host(vext)
        maps.append(m)
    res = run_bass_kernel_spmd(_NC_CACHE["attn"], maps, core_ids=list(range(NCORES)))
    return gather_T(res)


RG_PAIRS = [[0, 1], [2, 3], [4, 5], [6, 7]]
MIX_SHAPES = {"g": [128, DC], "b_u": [128, 4], "b_ag": [128, 8], "b_v_bc": [128, AW], "vln_g": [128, 4], "vln_b": [128, 4],
              "wsT": [128, 8, 128], "maskT": [128, 128], "bs_bc": [128, 4, 128], "cw": [128, 4, CK], "cb": [128, 4],
              "cln_g": [128, 4], "cln_b": [128, 4], "w_in": [D, 2048], "w_out": [D, D]}
FFN_SHAPES = {"g": [128, DC], "wup": [D, 2 * DFF], "cw": [128, 3, 2 * FC], "cb": [128, 2 * FC], "wdn": [DFF, D]}


def exchange_tail(C, xres, XH, bx_last, bxh, snd, rcv, flag, bflag, key):
    E = C.E
    bs = E.buf()
    br = E.buf()
    E.op("sp", lambda e: e.dma_start(out=snd.rearrange("(c p) t -> p c t", p=128), in_=xres[:, :, XH + T - 2:XH + T]),
         reads=[bx_last], writes=[bs], dmasem=key + "s")
    E.op("pool", lambda e: e.collective_compute("AllGather", ALU.bypass, replica_groups=RG_PAIRS, ins=[snd], outs=[rcv]),
         reads=[bs], writes=[br], dmasem=key + "c", inc=1)
    E.op("sp", lambda e: e.dma_start(out=xres[:, :, XH - 2:XH], in_=rcv[0:D, :].rearrange("(c p) t -> p c t", p=128)),
         reads=[br], writes=[bxh], dmasem=key + "l")
    E.op("dve", lambda e: e.tensor_scalar(xres[:, :, XH - 2:XH], xres[:, :, XH - 2:XH], flag[:, 0:1], None, ALU.mult),
         reads=[bflag, bxh], writes=[bxh])


def build_fused():
    nc = bass.Bass("TRN2", target_bir_lowering=False, num_devices=NCORES)
    xT = dram_in(nc, "xT", [D, XH0 + T])
    flag_d = dram_in(nc, "flag", [128, 1])
    Wm = {k: dram_in(nc, "m_" + k, sh) for k, sh in MIX_SHAPES.items()}
    Wm["flag"] = flag_d
    Wf = [{k: dram_in(nc, "f%d_%s" % (i, k), sh) for k, sh in FFN_SHAPES.items()} for i in range(2)]
    fg = dram_in(nc, "fg", [128, DC])
    Wa = attn_weight_aps(nc)
    Wa["flag"] = flag_d
    outT = dram_out(nc, "outT", [D, T])
    tls = [nc.dram_tensor("tl_s%d" % i, [D, 2], F32).ap() for i in range(2)]
    tlr = [nc.dram_tensor("tl_r%d" % i, [2 * D, 2], F32).ap() for i in range(2)]
    NX = 4
    ksnd = [nc.dram_tensor("ksnd%d" % i, [256, T], BF16).ap() for i in range(NX)]
    vsnd = [nc.dram_tensor("vsnd%d" % i, [256, T], BF16).ap() for i in range(NX)]
    kall = [nc.dram_tensor("kall%d" % i, [512, T], BF16).ap() for i in range(NX)]
    vall = [nc.dram_tensor("vall%d" % i, [512, T], BF16).ap() for i in range(NX)]
    own = lambda lst: (lambda c: lst[c // 2][(c % 2) * 128:(c % 2) * 128 + 128, :])
    with ExitStack() as st:
        C = make_ctx(nc, st)
        E = C.E
        xres, bx, bxh = load_xres(C, xT, XH0)
        flag = C.gsb("flag_sb", [128, 1], F32)
        bflag = E.buf("flag")
        E.op("sp", lambda e: e.dma_start(out=flag[:], in_=flag_d), writes=[bflag], dmasem="dflag")
        phase_begin(C)
        mixer_phase(C, xres, XH0, bx, bxh, Wm)
        phase_end(C)
        exchange_tail(C, xres, XH0, bx[NT - 1], bxh, tls[0], tlr[0], flag, bflag, "e1")
        phase_begin(C, psum="ffn")
        ffn_phase(C, xres, XH0, bx, bxh, Wf[0])
        phase_end(C)
        with ExitStack() as ast:
            qT = ast.enter_context(nc.sbuf_tensor("qT", [128, NH2, T], BF16))
            bqT = E.buf("qT")
            bks = E.buf("ks")
            bvs = E.buf("vs")
            phase_begin(C)
            qkv_phase(C, xres, XH0, bx, Wa, qT, bqT, own(ksnd), own(vsnd), bks, bvs)
            phase_end(C)
            bka = E.buf()
            bva = E.buf()
            for i in range(NX):
                E.op("pool", lambda e, i=i: e.collective_compute("AllGather", ALU.bypass, replica_groups=RG_PAIRS, ins=[ksnd[i]], outs=[kall[i]]),
                     writes=[bka], dmasem="cck", inc=1, group=True)
                E.op("pool", lambda e, i=i: e.collective_compute("AllGather", ALU.bypass, replica_groups=RG_PAIRS, ins=[vsnd[i]], outs=[vall[i]]),
                     writes=[bva], dmasem="ccv", inc=1, group=True)
            kv_events = lambda c: {"cck": c // 2 + 1, "ccv": c // 2 + 1}
            oT = ast.enter_context(nc.sbuf_tensor("oT", [128, NH2, T], BF16))
            boT = E.buf("oT")
            phase_begin(C, psum=False)
            attn_phase(C, qT, bqT, own(kall), own(ksnd), own(vall), own(vsnd), kv_events, oT, boT, Wa)
            phase_end(C)
            phase_begin(C)
            oproj_phase(C, xres, XH0, bx, oT, boT, Wa["wo"])
            phase_end(C)
        exchange_tail(C, xres, XH0, bx[NT - 1], bxh, tls[1], tlr[1], flag, bflag, "e3")
        phase_begin(C, psum="ffn")
        ffn_phase(C, xres, XH0, bx, bxh, Wf[1], final_g=fg, outT=outT)
        phase_end(C)
    return nc


def ffn_consts(inp, i):
    cw = inp["ffn_conv_w"][i]
    return {"g": pvec(inp["ffn_norm_g"][i]), "wup": np.ascontiguousarray(inp["ffn_w_up"][i]),
            "cw": np.ascontiguousarray(cw.reshape(3, 2 * FC, 128).transpose(2, 0, 1)), "cb": pvec(inp["ffn_conv_b"][i]),
            "wdn": np.ascontiguousarray(inp["ffn_w_down"][i])}


def kernel(**inputs):
    inp = {k: np.asarray(v) for k, v in inputs.items()}
    x = inp["x"]
    if "fused" not in _NC_CACHE:
        _NC_CACHE["fused"] = build_fused()
    base = {}
    for k, v in mixer_consts(inp).items():
        base["m_" + k] = v
    for i in range(2):
        for k, v in ffn_consts(inp, i).items():
            base["f%d_%s" % (i, k)] = v
    base["fg"] = pvec(inp["final_norm_g"])
    xs = xT_maps(x, XH0)
    maps = []
    for core in range(NCORES):
        h = core % 2
        m = dict(base)
        ac = attn_host_consts(inp, h)
        ac.pop("a_flag")
        m.update(ac)
        m["xT"] = xs[core]
        m["flag"] = np.full((128, 1), float(h), np.float32)
        maps.append(m)
    res = run_bass_kernel_spmd(_NC_CACHE["fused"], maps, core_ids=list(range(NCORES)))
    return gather_T(res).astype(np.float32)
```
